# Optimizing a Trainium2 kernel written in Bass

```python
import math
import jax, jax.numpy as jnp
from jax import lax
import numpy as np

D_MODEL = 1024
BATCH = 4
SEQ = 8192
DEPTH = 2

GRID_W = 64
CTX_LEN = 256
ROPE_BASE = 10000.0
ROPE_DIM = 64
ROPE_FREQS = ROPE_DIM // 4
NORM_EPS = 1e-6
MASK_VALUE = -1e30

DIFF_HEADS = 4
DIFF_DIM = 64
DIFF_VDIM = 2 * DIFF_DIM
DIFF_QBLOCK = 128

HGRN_HEADS = 4
HGRN_DK = 128
HGRN_DV = 128
HGRN_CHUNK = 64

SWA_Q_HEADS = 8
SWA_KV_HEADS = 2
SWA_GROUP = SWA_Q_HEADS // SWA_KV_HEADS
SWA_DIM = 64
WINDOW = 128
SWA_BLOCK = 128

BRANCH_WIDTH = 512
N_BRANCHES = 3
D_FF = 4 * D_MODEL

SPLIT_SIZES = (DIFF_HEADS * 2 * DIFF_DIM, DIFF_HEADS * 2 * DIFF_DIM, DIFF_HEADS * DIFF_VDIM,
               HGRN_HEADS * HGRN_DK, HGRN_HEADS * HGRN_DK, HGRN_HEADS * HGRN_DK,
               HGRN_HEADS * HGRN_DV, HGRN_HEADS * HGRN_DV,
               SWA_Q_HEADS * SWA_DIM, SWA_KV_HEADS * SWA_DIM, SWA_KV_HEADS * SWA_DIM,
               N_BRANCHES * D_MODEL)
IN_WIDTH = sum(SPLIT_SIZES)

F32 = jnp.float32

kernel_name = "hybrid_diffattn_hgrn2_swa_dit"


def rms_norm(x, g):
    xf = x.astype(F32)
    y = xf * lax.rsqrt(jnp.mean(xf * xf, axis=-1, keepdims=True) + NORM_EPS)
    return (y * g.astype(F32)).astype(x.dtype)


def modulate(h, shift, scale):
    return h * (1.0 + scale) + shift


def axial_rope_tables(n_tokens):
    rows = n_tokens // GRID_W
    row = jnp.repeat(jnp.arange(rows), GRID_W)
    col = jnp.tile(jnp.arange(GRID_W), rows)
    inv_freq = ROPE_BASE ** (-jnp.arange(ROPE_FREQS, dtype=F32) / ROPE_FREQS)
    pos = jnp.stack([row, col], axis=-1).astype(F32)
    ang = pos[:, :, None] * inv_freq
    return jnp.cos(ang), jnp.sin(ang)


def apply_axial_rope(x, cos, sin):
    shp = x.shape
    xf = x.astype(F32).reshape(shp[:-1] + (2, 2, ROPE_FREQS))
    mid = (1,) * (x.ndim - 3)
    cs = cos.reshape((cos.shape[0],) + mid + (2, ROPE_FREQS))
    sn = sin.reshape((sin.shape[0],) + mid + (2, ROPE_FREQS))
    x1, x2 = xf[..., 0, :], xf[..., 1, :]
    out = jnp.stack([x1 * cs - x2 * sn, x2 * cs + x1 * sn], axis=-2)
    return out.reshape(shp).astype(x.dtype)


def split_columns(w):
    idx = np.cumsum(SPLIT_SIZES)[:-1].tolist()
    return jnp.split(w, idx, axis=-1)


def heads(a, *hd):
    return a.reshape(a.shape[:2] + hd)


def diff_weights(q, k, lam):
    s = jnp.einsum('bqhcd,bkhcd->bhcqk', q, k).astype(F32) * (DIFF_DIM ** -0.5)
    p = jax.nn.softmax(s, axis=-1)
    return p[:, :, 0] - lam * p[:, :, 1]


def diff_attention(q_l, k_l, v_l, q_c, k_c, v_c, lam_vecs, subln, lam_init, need_ctx):
    lv = lam_vecs.astype(F32)
    lam = jnp.exp(jnp.sum(lv[0] * lv[1])) - jnp.exp(jnp.sum(lv[2] * lv[3])) + lam_init
    B, S = q_l.shape[:2]
    nb = S // DIFF_QBLOCK
    k_all = jnp.concatenate([k_c, k_l], axis=1)
    v_all = jnp.concatenate([v_c, v_l], axis=1)
    qb = jnp.moveaxis(q_l.reshape(B, nb, DIFF_QBLOCK, DIFF_HEADS, 2, DIFF_DIM), 1, 0)

    def block(qi):
        w = diff_weights(qi, k_all, lam)
        return jnp.einsum('bhqk,bkhe->bqhe', w.astype(v_all.dtype), v_all)

    o_l = jnp.moveaxis(lax.map(block, qb), 0, 1).reshape(B, S, DIFF_HEADS, DIFF_VDIM)
    out_scale = 1.0 - lam_init
    y_l = (rms_norm(o_l, subln) * out_scale).reshape(B, S, -1)
    y_c = None
    if need_ctx:
        w_c = diff_weights(q_c, k_c, lam)
        o_c = jnp.einsum('bhqk,bkhe->bqhe', w_c.astype(v_c.dtype), v_c)
        y_c = (rms_norm(o_c, subln) * out_scale).reshape(o_c.shape[0], o_c.shape[1], -1)
    return y_l, y_c


def hgrn_gates(z, lb):
    z = z.astype(F32)
    k = (1.0 - lb) * jax.nn.sigmoid(-z)
    logf = jnp.log1p(-k)
    return logf, k


def gla_chunk_scan(q, k, v, logf, s0):
    B, L, H, _ = q.shape
    dv = v.shape[-1]
    n = L // HGRN_CHUNK

    def chunks(a):
        return jnp.moveaxis(a.reshape(B, n, HGRN_CHUNK, H, a.shape[-1]), 1, 0)

    causal = jnp.tril(jnp.ones((HGRN_CHUNK, HGRN_CHUNK), dtype=bool))[:, :, None, None]

    def step(S, inp):
        qc, kc, vc, lc = inp
        b = jnp.cumsum(lc, axis=1)
        o_inter = jnp.einsum('bthk,bhkv->bthv', qc * jnp.exp(b), S)
        rel = jnp.where(causal, b[:, :, None] - b[:, None, :], 0.0)
        decay = jnp.where(causal, jnp.exp(rel), 0.0)
        A = jnp.einsum('bthk,bshk,btshk->bhts', qc, kc, decay)
        o_intra = jnp.einsum('bhts,bshv->bthv', A, vc)
        b_last = b[:, -1]
        S_new = S * jnp.exp(b_last)[..., None] + jnp.einsum(
            'bshk,bshv->bhkv', kc * jnp.exp(b_last[:, None] - b), vc)
        return S_new, o_inter + o_intra

    S_fin, o = lax.scan(step, s0, (chunks(q), chunks(k), chunks(v), chunks(logf)))
    return jnp.moveaxis(o, 0, 1).reshape(B, L, H, dv), S_fin


def hgrn_direction(q_c, z_c, v_c, q_l, z_l, v_l, lb):
    logf_c, k_c = hgrn_gates(z_c, lb)
    logf_l, k_l = hgrn_gates(z_l, lb)
    B, _, H, dk = q_c.shape
    s0 = jnp.zeros((B, H, dk, v_c.shape[-1]), F32)
    o_c, s_c = gla_chunk_scan(q_c, k_c, v_c, logf_c, s0)
    o_l, _ = gla_chunk_scan(q_l, k_l, v_l, logf_l, s_c)
    return o_c, o_l


def hgrn2(q_l, zf_l, zb_l, v_l, g_l, q_c, zf_c, zb_c, v_c, g_c, lb, norm_w, need_ctx):
    q_l, q_c = jax.nn.silu(q_l.astype(F32)), jax.nn.silu(q_c.astype(F32))
    v_l, v_c = v_l.astype(F32), v_c.astype(F32)
    rev = lambda a: a[:, ::-1]
    oc_f, ol_f = hgrn_direction(q_c, zf_c, v_c, q_l, zf_l, v_l, lb[0])
    oc_b, ol_b = hgrn_direction(rev(q_c), rev(zb_c), rev(v_c), rev(q_l), rev(zb_l), rev(v_l), lb[1])

    def readout(o, g):
        y = rms_norm(o, norm_w) * jax.nn.silu(g.astype(F32))
        return y.reshape(y.shape[:2] + (-1,)).astype(g.dtype)

    y_l = readout(ol_f + rev(ol_b), g_l)
    y_c = readout(oc_f + rev(oc_b), g_c) if need_ctx else None
    return y_l, y_c


def swa_latent(q, k, v, k_c, v_c, sink):
    B, S = q.shape[:2]
    C = k_c.shape[1]
    nb = S // SWA_BLOCK
    scale = SWA_DIM ** -0.5
    qb = jnp.moveaxis(q.reshape(B, nb, SWA_BLOCK, SWA_KV_HEADS, SWA_GROUP, SWA_DIM), 1, 0)
    pad = ((0, 0), (SWA_BLOCK, SWA_BLOCK), (0, 0), (0, 0))
    kp, vp = jnp.pad(k, pad), jnp.pad(v, pad)
    qi = jnp.arange(SWA_BLOCK)[:, None]
    kj = jnp.arange(3 * SWA_BLOCK)[None, :]
    band = jnp.abs(kj - SWA_BLOCK - qi) <= WINDOW
    sink_l = sink.astype(F32).reshape(SWA_KV_HEADS, SWA_GROUP, 1, 1)

    def block(args):
        n, qn = args
        start = n * SWA_BLOCK
        kw = lax.dynamic_slice_in_dim(kp, start, 3 * SWA_BLOCK, axis=1)
        vw = lax.dynamic_slice_in_dim(vp, start, 3 * SWA_BLOCK, axis=1)
        kpos = start - SWA_BLOCK + kj
        valid = band & (kpos >= 0) & (kpos < S)
        s_win = jnp.einsum('bqhgd,bkhd->bhgqk', qn, kw).astype(F32) * scale
        s_win = jnp.where(valid, s_win, MASK_VALUE)
        s_ctx = jnp.einsum('bqhgd,bkhd->bhgqk', qn, k_c).astype(F32) * scale
        sk = jnp.broadcast_to(sink_l, s_ctx.shape[:-1] + (1,))
        p = jax.nn.softmax(jnp.concatenate([s_ctx, s_win, sk], axis=-1), axis=-1)
        p_ctx = p[..., :C].astype(v.dtype)
        p_win = p[..., C:C + 3 * SWA_BLOCK].astype(v.dtype)
        return (jnp.einsum('bhgqk,bkhd->bqhgd', p_ctx, v_c)
                + jnp.einsum('bhgqk,bkhd->bqhgd', p_win, vw))

    o = lax.map(block, (jnp.arange(nb), qb))
    return jnp.moveaxis(o, 0, 1).reshape(B, S, SWA_Q_HEADS * SWA_DIM)


def swa_context(q, k, v, sink):
    B, C = q.shape[:2]
    qh = q.reshape(B, C, SWA_KV_HEADS, SWA_GROUP, SWA_DIM)
    s = jnp.einsum('bqhgd,bkhd->bhgqk', qh, k).astype(F32) * (SWA_DIM ** -0.5)
    sk = jnp.broadcast_to(sink.astype(F32).reshape(SWA_KV_HEADS, SWA_GROUP, 1, 1), s.shape[:-1] + (1,))
    p = jax.nn.softmax(jnp.concatenate([s, sk], axis=-1), axis=-1)[..., :C]
    o = jnp.einsum('bhgqk,bkhd->bqhgd', p.astype(v.dtype), v)
    return o.reshape(B, C, SWA_Q_HEADS * SWA_DIM)


def merge_branches(ys, gates, w_branch, w_out):
    g = jax.nn.sigmoid(gates.reshape(gates.shape[:2] + (N_BRANCHES, D_MODEL)))
    merged = g[..., 0, :] * (ys[0] @ w_branch[0])
    for j in range(1, N_BRANCHES):
        merged = merged + g[..., j, :] * (ys[j] @ w_branch[j])
    return merged @ w_out


def token_mixer(h_l, h_c, w_in, lam_vecs, subln, lam_init, lb, hgrn_w, sink,
                w_branch, w_out, cos, sin, need_ctx):
    w_parts = split_columns(w_in)
    (dq_l, dk_l, dv_l, hq_l, hff_l, hfb_l, hi_l, hg_l, sq_l, sk_l, sv_l, gt_l) = [h_l @ w for w in w_parts]
    (dq_c, dk_c, dv_c, hq_c, hff_c, hfb_c, hi_c, hg_c, sq_c, sk_c, sv_c, gt_c) = [h_c @ w for w in w_parts]

    y_a_l, y_a_c = diff_attention(
        apply_axial_rope(heads(dq_l, DIFF_HEADS, 2, DIFF_DIM), cos, sin),
        apply_axial_rope(heads(dk_l, DIFF_HEADS, 2, DIFF_DIM), cos, sin),
        heads(dv_l, DIFF_HEADS, DIFF_VDIM),
        heads(dq_c, DIFF_HEADS, 2, DIFF_DIM), heads(dk_c, DIFF_HEADS, 2, DIFF_DIM),
        heads(dv_c, DIFF_HEADS, DIFF_VDIM), lam_vecs, subln, lam_init, need_ctx)

    hk = lambda a: heads(a, HGRN_HEADS, HGRN_DK)
    hv = lambda a: heads(a, HGRN_HEADS, HGRN_DV)
    y_b_l, y_b_c = hgrn2(hk(hq_l), hk(hff_l), hk(hfb_l), hv(hi_l), hv(hg_l),
                         hk(hq_c), hk(hff_c), hk(hfb_c), hv(hi_c), hv(hg_c), lb, hgrn_w, need_ctx)

    k_c = heads(sk_c, SWA_KV_HEADS, SWA_DIM)
    v_c = heads(sv_c, SWA_KV_HEADS, SWA_DIM)
    y_c_l = swa_latent(apply_axial_rope(heads(sq_l, SWA_Q_HEADS, SWA_DIM), cos, sin),
                       apply_axial_rope(heads(sk_l, SWA_KV_HEADS, SWA_DIM), cos, sin),
                       heads(sv_l, SWA_KV_HEADS, SWA_DIM), k_c, v_c, sink)

    out_l = merge_branches((y_a_l, y_b_l, y_c_l), gt_l, w_branch, w_out)
    out_c = None
    if need_ctx:
        y_c_c = swa_context(heads(sq_c, SWA_Q_HEADS, SWA_DIM), k_c, v_c, sink)
        out_c = merge_branches((y_a_c, y_b_c, y_c_c), gt_c, w_branch, w_out)
    return out_l, out_c


def sq_relu_mlp(h, w_up, w_down):
    return jnp.square(jax.nn.relu(h @ w_up)) @ w_down


def setup_inputs(seed: int = 0) -> dict:
    key = jax.random.key(seed)
    ks = jax.random.split(key, 17)

    def nrm(k, shape, scale):
        return jax.random.normal(k, shape, F32) * scale

    return {
        "x": nrm(ks[0], (BATCH, SEQ, D_MODEL), 1.0),
        "c": nrm(ks[1], (BATCH, D_MODEL), 1.0),
        "ctx": nrm(ks[2], (BATCH, CTX_LEN, D_MODEL), 1.0),
        "c_ctx": nrm(ks[3], (D_MODEL,), 1.0),
        "w_ada": nrm(ks[4], (DEPTH, D_MODEL, 6 * D_MODEL), 0.5 * D_MODEL ** -0.5),
        "b_ada": nrm(ks[5], (DEPTH, 6 * D_MODEL), 0.01),
        "norm_g": 1.0 + nrm(ks[6], (DEPTH, 4, D_MODEL), 0.02),
        "w_in": nrm(ks[7], (DEPTH, D_MODEL, IN_WIDTH), D_MODEL ** -0.5),
        "diff_lambda": nrm(ks[8], (DEPTH, 4, DIFF_DIM), 0.1),
        "diff_subln": 1.0 + nrm(ks[9], (DEPTH, DIFF_VDIM), 0.02),
        "hgrn_lb": nrm(ks[10], (DEPTH, 2, HGRN_HEADS * HGRN_DK), 0.1),
        "hgrn_norm": 1.0 + nrm(ks[11], (DEPTH, HGRN_DV), 0.02),
        "swa_sink": nrm(ks[12], (DEPTH, SWA_Q_HEADS), 0.5),
        "w_branch": nrm(ks[13], (DEPTH, N_BRANCHES, BRANCH_WIDTH, D_MODEL), BRANCH_WIDTH ** -0.5),
        "w_out": nrm(ks[14], (DEPTH, D_MODEL, D_MODEL), D_MODEL ** -0.5),
        "w_mlp_up": nrm(ks[15], (DEPTH, D_MODEL, D_FF), D_MODEL ** -0.5),
        "w_mlp_down": nrm(ks[16], (DEPTH, D_FF, D_MODEL), D_FF ** -0.5),
    }


def reference(x, c, ctx, c_ctx, w_ada, b_ada, norm_g, w_in, diff_lambda, diff_subln,
              hgrn_lb, hgrn_norm, swa_sink, w_branch, w_out, w_mlp_up, w_mlp_down):
    cos, sin = axial_rope_tables(x.shape[1])
    lb_soft = jax.nn.softmax(hgrn_lb.astype(F32), axis=0)
    lower_bounds = (jnp.cumsum(lb_soft, axis=0) - lb_soft[0:1]).reshape(
        lb_soft.shape[:2] + (HGRN_HEADS, HGRN_DK))
    c_act = jax.nn.silu(c)
    cc_act = jax.nn.silu(c_ctx)
    for l in range(DEPTH):
        need_ctx = l < DEPTH - 1
        lam_init = 0.8 - 0.6 * math.exp(-0.3 * l)
        m_l = jnp.split((c_act @ w_ada[l] + b_ada[l])[:, None, :], 6, axis=-1)
        m_c = jnp.split((cc_act @ w_ada[l] + b_ada[l])[None, None, :], 6, axis=-1)

        h_l = modulate(rms_norm(x, norm_g[l, 0]), m_l[0], m_l[1])
        h_c = modulate(rms_norm(ctx, norm_g[l, 0]), m_c[0], m_c[1])
        y_l, y_c = token_mixer(h_l, h_c, w_in[l], diff_lambda[l], diff_subln[l], lam_init,
                               lower_bounds[l], hgrn_norm[l], swa_sink[l], w_branch[l], w_out[l],
                               cos, sin, need_ctx)
        x = x + m_l[2] * rms_norm(y_l, norm_g[l, 1])
        h_l = modulate(rms_norm(x, norm_g[l, 2]), m_l[3], m_l[4])
        x = x + m_l[5] * rms_norm(sq_relu_mlp(h_l, w_mlp_up[l], w_mlp_down[l]), norm_g[l, 3])

        if need_ctx:
            ctx = ctx + m_c[2] * rms_norm(y_c, norm_g[l, 1])
            h_c = modulate(rms_norm(ctx, norm_g[l, 2]), m_c[3], m_c[4])
            ctx = ctx + m_c[5] * rms_norm(sq_relu_mlp(h_c, w_mlp_up[l], w_mlp_down[l]), norm_g[l, 3])
    return x
```

```python
import math
import numpy as np
import ml_dtypes
from contextlib import ExitStack
import concourse.bass as bass
import concourse.mybir as mybir
from concourse.bass_utils import run_bass_kernel_spmd

F32 = mybir.dt.float32
BF16 = mybir.dt.bfloat16
AF = mybir.ActivationFunctionType
ALU = mybir.AluOpType
AX = mybir.AxisListType

ENGS = ("pe", "act", "dve", "pool", "sp")
DMAQ = ("sp", "pool", "act")
KD = 6
ST_Q = "pool"


class Res:
    __slots__ = ("w", "r")

    def __init__(self):
        self.w = None
        self.r = {}


class _Rec:
    def __init__(self):
        self.call = None

    def __getattr__(self, name):
        def f(*a, **k):
            self.call = (name, a, k)
            return self
        return f


def _replay(call):
    name, a, k = call
    return lambda e: getattr(e, name)(*a, **k)


class Sched:
    def __init__(self, nc, stack):
        self.nc = nc
        self.sem = {e: stack.enter_context(nc.semaphore("s_" + e)) for e in ENGS}
        self.dsem = {}
        for q in DMAQ:
            for k in range(KD):
                self.dsem[(q, k)] = stack.enter_context(nc.semaphore("d_%s%d" % (q, k)))
        self.base = {e: 0 for e in ENGS}
        self.dcount = {k: 0 for k in self.dsem}
        self.dstart = dict(self.dcount)
        self.drr = {q: 0 for q in DMAQ}
        self.phase = 0
        self.nops = 0
        self._reset()

    def _reset(self):
        self.ops = {e: [] for e in ENGS}
        self.seen = {e: {} for e in ENGS}

    def _collect(self, eng, reads, writes):
        waits = {}
        seen = self.seen[eng]
        ph = self.phase

        def need(tag, same_ok):
            if tag is None or tag[0] != ph:
                return
            sid, val = tag[1]
            if same_ok and sid == eng:
                return
            if seen.get(sid, 0) >= val:
                return
            if waits.get(sid, 0) < val:
                waits[sid] = val

        for r in reads:
            need(r.w, False)
        for w in writes:
            need(w.w, True)
            for tag in w.r.values():
                need(tag, True)
        for sid, val in waits.items():
            seen[sid] = val
        return waits

    def _commit(self, dep, reads, writes):
        tag = (self.phase, dep)
        for r in reads:
            r.r[dep[0]] = tag
        for w in writes:
            w.w = tag
            w.r = {}

    def op(self, eng, fn, reads=(), writes=()):
        waits = self._collect(eng, reads, writes)
        rec = _Rec()
        fn(rec)
        assert rec.call is not None
        fn = _replay(rec.call)
        self.ops[eng].append([fn, waits, None, False])
        self._commit((eng, len(self.ops[eng])), reads, writes)
        self.nops += 1

    def dma(self, q, out, in_, reads=(), writes=(), **kw):
        if q == "sp" and len(writes) == 0:
            q = ST_Q
        k = self.drr[q] % KD
        self.drr[q] += 1
        slot = (q, k)
        sid = ("d", q, k)
        waits = self._collect(q, reads, writes)
        prev = 16 * self.dcount[slot]
        if prev > 16 * self.dstart[slot] and self.seen[q].get(sid, 0) < prev:
            waits[sid] = max(waits.get(sid, 0), prev)
            self.seen[q][sid] = prev
        self.dcount[slot] += 1
        val = 16 * self.dcount[slot]
        self.ops[q].append([lambda e: e.dma_start(out=out, in_=in_, **kw), waits, slot, False])
        self._commit((sid, val), reads, writes)
        self.nops += 1

    def flush(self, final=False):
        nc = self.nc
        tgt = {e: set() for e in ENGS}
        for e in ENGS:
            for o in self.ops[e]:
                for sid, val in o[1].items():
                    if isinstance(sid, str):
                        tgt[sid].add(val)
        for e in ENGS:
            for i in range(len(self.ops[e]), 0, -1):
                if self.ops[e][i - 1][2] is None:
                    tgt[e].add(i)
                    break
        semval = {}
        newbase = {}
        for e in ENGS:
            v = self.base[e]
            m = {}
            for i, o in enumerate(self.ops[e], start=1):
                if i in tgt[e]:
                    assert o[2] is None
                    v += 1
                    m[i] = v
                    o[3] = True
            semval[e] = m
            newbase[e] = v
        base = dict(self.base)
        dstart = dict(self.dstart)
        dend = dict(self.dcount)
        ops = self.ops
        sem = self.sem
        dsem = self.dsem

        def body(e, ename):
            for f in ENGS:
                if base[f] > 0 and f != ename:
                    e.wait_ge(sem[f], base[f])
            for slot, cnt in dstart.items():
                if cnt > 0:
                    e.wait_ge(dsem[slot], 16 * cnt)
            for o in ops[ename]:
                for sid, val in o[1].items():
                    if isinstance(sid, str):
                        e.wait_ge(sem[sid], semval[sid][val])
                    else:
                        e.wait_ge(dsem[(sid[1], sid[2])], val)
                ins = o[0](e)
                if o[2] is not None:
                    ins.then_inc(dsem[o[2]], 16)
                elif o[3]:
                    ins.then_inc(sem[ename], 1)
            if final:
                for f in ENGS:
                    if newbase[f] > 0 and f != ename:
                        e.wait_ge(sem[f], newbase[f])
                for slot, cnt in dend.items():
                    if cnt > 0:
                        e.wait_ge(dsem[slot], 16 * cnt)

        with nc.Block() as block:
            @block.tensor
            def _(e):
                body(e, "pe")

            @block.scalar
            def _(e):
                body(e, "act")

            @block.vector
            def _(e):
                body(e, "dve")

            @block.gpsimd
            def _(e):
                body(e, "pool")

            @block.sync
            def _(e):
                body(e, "sp")

        self.base = newbase
        self.dstart = dend
        self.phase += 1
        self._reset()


_UID = [0]


class Ring:
    def __init__(self, nc, st, name, n, shape, dtype, psum=False):
        _UID[0] += 1
        if not psum:
            self.t = [st.enter_context(nc.sbuf_tensor("%s_%d_%d" % (name, _UID[0], i), shape, dtype)) for i in range(n)]
        else:
            isz = 2 if dtype == BF16 else 4
            full = [st.enter_context(nc.psum_tensor("%s_%d_%d" % (name, _UID[0], i), [128, 2048 // isz], dtype))
                    for i in range(n)]
            self.t = []
            for f in full:
                fs = 1
                for d_ in shape[1:]:
                    fs *= d_
                v = f[0:shape[0], 0:fs]
                if len(shape) == 3:
                    v = v.rearrange("p (a b) -> p a b", a=shape[1])
                self.t.append(v)
        self.r = [Res() for _ in range(n)]
        self.i = 0
        self.n = n

    def next(self):
        k = self.i % self.n
        self.i += 1
        return self.t[k], self.r[k]


D = 1024
CTX = 256
DEPTH = 2
EPS = 1e-6
NSB = 62
NMC = 1664
SB_KINDS = (["dq"] * 8 + ["dk"] * 8 + ["sq"] * 8 + ["sk"] * 2 +
            ["hq"] * 4 + ["zf"] * 4 + ["zb"] * 4 + ["gate"] * 24)


def w_in_column_maps():
    def rot(cols):
        cols = np.asarray(cols)
        return cols ^ 16
    s = []
    for base, nblk in ((0, 4), (512, 4), (4096, 4), (4608, 1)):
        for b in range(nblk):
            c = np.arange(base + b * 128, base + (b + 1) * 128)
            s.append(c)
            s.append(base + ((c - base) ^ 16))
    for base, nblk in ((1536, 4), (2048, 4), (2560, 4), (4864, 24)):
        for b in range(nblk):
            s.append(np.arange(base + b * 128, base + (b + 1) * 128))
    s = np.concatenate(s)
    m = np.concatenate([np.arange(1024, 1536), np.arange(3072, 3584), np.arange(3584, 4096),
                        np.arange(4736, 4864)])
    assert s.size == NSB * 128 and m.size == NMC
    return s, m


def rope_tables(SEQ):
    T = CTX + SEQ
    t = np.arange(SEQ)
    pos = np.stack([t // 64, t % 64], axis=-1).astype(np.float32)
    inv = (10000.0 ** (-np.arange(16, dtype=np.float32) / 16)).astype(np.float32)
    ang = pos[:, :, None] * inv
    cos = np.cos(ang).astype(np.float32)
    sin = np.sin(ang).astype(np.float32)
    d = np.arange(128) % 64
    half = d // 32
    part = (d % 32) // 16
    f = d % 16
    C = np.ones((128, T), np.float32)
    S_ = np.zeros((128, T), np.float32)
    C[:, CTX:] = cos[:, half, f].T
    sign = np.where(part == 0, -1.0, 1.0).astype(np.float32)
    S_[:, CTX:] = (sin[:, half, f].T) * sign[:, None]
    return np.stack([C, S_, C * 0.125, S_ * 0.125]).astype(np.float32)


def build_nc(SEQ, dbg=False, stop=None):
    T = CTX + SEQ
    NT = T // 128
    NCH = T // 64
    L = DEPTH
    nc = bass.Bass("TRN2", target_bir_lowering=False)

    def din(name, shape, dt=F32):
        return nc.dram_tensor(name, list(shape), dt, kind="ExternalInput").ap()

    def dscr(name, shape, dt):
        if dbg:
            return nc.dram_tensor(name, list(shape), dt, kind="ExternalOutput").ap()
        return nc.dram_tensor(name, list(shape), dt).ap()

    xcat = din("xcat", [T, D])
    c_in = din("c2", [2, D])
    w_ada = din("w_ada", [L, D, 6 * D])
    b_ada = din("b_ada", [L, 6 * D])
    norm_g = din("norm_g", [L, 4 * D])
    w_in_s = din("w_in_s", [L, D, NSB * 128])
    w_in_m = din("w_in_m", [L, D, NMC])
    dlam = din("dlam", [L, 256])
    dsub = din("dsub", [L, 128])
    hlb = din("hlb", [L, 2, 512])
    hnorm = din("hnorm", [L, 128])
    sink = din("sink", [L, 8])
    w_br = din("w_br", [L, 3, 512, D])
    w_out = din("w_out", [L, D, D])
    w_up = din("w_up", [L, D, 4 * D])
    w_dn = din("w_dn", [L, 4 * D, D])
    rope = din("rope", [4, 128, T])
    ident_d = din("ident", [128, 128], BF16)
    masks_d = din("masks", [5, 128, 128], BF16)
    reset_d = din("reset", [128, 512])
    out = nc.dram_tensor("out", [SEQ, D], F32, kind="ExternalOutput").ap()

    XS = dscr("XS", [T, D], F32)
    X1 = dscr("X1", [T, D], F32)
    H2T = dscr("H2T", [128, 8, T], BF16)
    WS_in = dscr("WS_in", [L, NSB, 128, 8, 128], BF16)
    WM_in = dscr("WM_in", [L, 128, 8, NMC], BF16)
    WS_br = dscr("WS_br", [L, 3, 8, 128, 4, 128], BF16)
    WM_out = dscr("WM_out", [L, 128, 8, D], BF16)
    WS_up = dscr("WS_up", [L, 32, 128, 8, 128], BF16)
    WM_dn = dscr("WM_dn", [L, 128, 32, D], BF16)
    QT = dscr("QT", [512, T], BF16)
    KT = dscr("KT", [512, T], BF16)
    SQT = dscr("SQT", [512, T], BF16)
    SKT = dscr("SKT", [128, T], BF16)
    HQT = dscr("HQT", [512, T], F32)
    SGT = dscr("SGT", [2, 512, T], F32)
    GT = dscr("GT", [3072, T], BF16)
    DV = dscr("DV", [T, 4, 129], BF16)
    HV = dscr("HV", [T, 512], BF16)
    HG = dscr("HG", [T, 512], F32)
    SV = dscr("SV", [T, 2, 65], BF16)
    YAT = dscr("YAT", [512, T], BF16)
    YB = dscr("YB", [T, 512], BF16)
    YC = dscr("YC", [T, 512], BF16)
    SPD = dscr("SPD", [2, NCH, 128, 128], BF16)
    DBGQ = dscr("DBGQ", [4, 2, 128, T], BF16)
    DBG1 = dscr("DBG1", [4, 128, 8], F32)
    DBG2 = dscr("DBG2", [4, 128, 128], F32)
    DBG3 = dscr("DBG3", [4, 2, 128, 129], F32)
    MODD = dscr("MODD", [128, 2, 6, D], F32)

    supers = [(0, CTX)] + [(CTX + 512 * i, 512) for i in range(SEQ // 512)]

    with ExitStack() as top:
        S = Sched(nc, top)

        def sbt(st, name, shape, dt):
            _UID[0] += 1
            return st.enter_context(nc.sbuf_tensor("%s_%d" % (name, _UID[0]), list(shape), dt))

        IDN = sbt(top, "IDN", [128, 128], BF16)
        MSK = sbt(top, "MSK", [128, 5, 128], BF16)
        LAM = sbt(top, "LAM", [128, 4], F32)
        SUBC = sbt(top, "SUBC", [128, 1], F32)
        HNW = sbt(top, "HNW", [128, 128], F32)
        ESK = sbt(top, "ESK", [128, 8], F32)
        OML = sbt(top, "OML", [128, 8], F32)
        rC = Res()

        S.dma("sp", IDN[:], ident_d, writes=[rC])
        S.dma("sp", MSK[:], masks_d.rearrange("m p c -> p m c"), writes=[rC])
        S.flush()

        def phase_W(l):
            with ExitStack() as st:
                fr = Ring(nc, st, "wf", 2, [128, 4096], F32)
                br = Ring(nc, st, "wb", 2, [128, 4096], BF16)
                jobs = []
                def stat_jobs(wsrc2d, dst5, nblk, kc, grp):
                    src = wsrc2d.rearrange("(k p) n -> p k n", p=128)
                    for b0 in range(0, nblk, grp):
                        nb = min(grp, nblk - b0)
                        n = kc * nb * 128
                        jobs.append((src[:, :, b0 * 128:(b0 + nb) * 128], n,
                                     ("p (k n) -> p k n", dict(k=kc)),
                                     ("p (k b c) -> p k b c", dict(k=kc, b=nb)),
                                     ("p (b k c) -> p k b c", dict(k=kc, b=nb)),
                                     ("p (b x) -> p b x", dict(b=nb)),
                                     dst5[b0:b0 + nb].rearrange("b p k c -> p b (k c)")))

                def mov_jobs(wsrc2d, dst3, kc_tot, kstep, ncols, cstep):
                    src = wsrc2d.rearrange("(k p) n -> p k n", p=128)
                    for k0 in range(0, kc_tot, kstep):
                        for c0 in range(0, ncols, cstep):
                            nc_ = min(cstep, ncols - c0)
                            n = kstep * nc_
                            v = ("p (k n) -> p k n", dict(k=kstep))
                            jobs.append((src[:, k0:k0 + kstep, c0:c0 + nc_], n, v, v, v, v,
                                         dst3[:, k0:k0 + kstep, c0:c0 + nc_]))

                stat_jobs(w_in_s[l], WS_in[l], NSB, 8, 4)
                mov_jobs(w_in_m[l], WM_in[l], 8, 8, NMC, 512)
                for j in range(3):
                    stat_jobs(w_br[l, j], WS_br[l, j], 8, 4, 8)
                mov_jobs(w_out[l], WM_out[l], 8, 8, D, 512)
                stat_jobs(w_up[l], WS_up[l], 32, 8, 4)
                mov_jobs(w_dn[l], WM_dn[l], 32, 4, D, 1024)
                for i, (sap, n, (pl, kl), (pci, kci), (pco, kco), (ps_, ks_), dap) in enumerate(jobs):
                    ft, fres = fr.next()
                    bt, bres = br.next()
                    S.dma("sp", ft[:, 0:n].rearrange(pl, **kl), sap, writes=[fres])
                    eng = "pool" if i % 2 == 0 else "dve"
                    S.op(eng, lambda e: e.tensor_copy(out=bt[:, 0:n].rearrange(pco, **kco),
                                                      in_=ft[:, 0:n].rearrange(pci, **kci)),
                         reads=[fres], writes=[bres])
                    S.dma("sp", dap, bt[:, 0:n].rearrange(ps_, **ks_), reads=[bres])
                S.flush()

        def phase_M(l):
            lam_init = 0.8 - 0.6 * math.exp(-0.3 * l)
            with ExitStack() as st:
                CT = sbt(st, "CT", [128, 16], F32)
                CA = sbt(st, "CA", [128, 16], F32)
                CAB = sbt(st, "CAB", [128, 16, 128], F32)
                BA = sbt(st, "BA", [128, 6 * D], F32)
                NG = sbt(st, "NG", [128, 4 * D], F32)
                RAW = sbt(st, "RAW", [128, 2, 6 * D], F32)
                war = Ring(nc, st, "wa", 2, [128, 8, 512], F32)
                pr = Ring(nc, st, "pm", 4, [128, 512], F32, psum=True)
                DL = sbt(st, "DL", [128, 256], F32)
                TL = sbt(st, "TL", [128, 128], F32)
                SC = sbt(st, "SC", [128, 8], F32)
                LB = sbt(st, "LB", [128, L, 8], F32)
                LE = sbt(st, "LE", [128, L, 8], F32)
                LS = sbt(st, "LS", [128, 8], F32)
                LR = sbt(st, "LR", [128, 8], F32)
                rT = Res(); rA = Res(); rB = Res(); rBA = Res(); rNG = Res(); rRAW = Res()
                rDL = Res(); rTL = Res(); rSC = Res(); rLB = Res(); rLE = Res(); rLS = Res()
                S.dma("sp", CT[:].rearrange("p (s k) -> p s k", s=2),
                      c_in.rearrange("s (k p) -> p s k", p=128), writes=[rT], allow_slow_non_contiguous=True)
                S.dma("sp", BA[:], b_ada[l:l + 1, :].partition_broadcast(128), writes=[rBA])
                S.dma("sp", NG[:], norm_g[l:l + 1, :].partition_broadcast(128), writes=[rNG])
                S.op("act", lambda e: e.activation(out=CA[:], in_=CT[:], func=AF.Silu), reads=[rT], writes=[rA])
                S.op("dve", lambda e: e.tensor_copy(out=CAB[:], in_=CA[:].unsqueeze(2).broadcast_to([128, 16, 128])),
                     reads=[rA], writes=[rB])
                wsrc = w_ada[l].rearrange("(k p) n -> p k n", p=128)
                for cb in range(12):
                    wt, wres = war.next()
                    S.dma("sp", wt[:], wsrc[:, :, cb * 512:(cb + 1) * 512], writes=[wres])
                    for s in range(2):
                        ps, pres = pr.next()
                        for k in range(8):
                            S.op("pe", (lambda ps=ps, wt=wt, s=s, k=k: lambda e: e.matmul(
                                ps[:], lhsT=CAB[:, s * 8 + k, :], rhs=wt[:, k, :], start=(k == 0), stop=(k == 7)))(),
                                reads=[rB, wres], writes=[pres])
                        S.op("dve", (lambda ps=ps, s=s, cb=cb: lambda e: e.tensor_tensor(
                            out=RAW[:, s, cb * 512:(cb + 1) * 512], in0=ps[:], in1=BA[:, cb * 512:(cb + 1) * 512],
                            op=ALU.add))(), reads=[pres, rBA], writes=[rRAW])
                for s in range(2):
                    m = lambda i, s=s: RAW[:, s, i * D:(i + 1) * D]
                    g = lambda i: NG[:, i * D:(i + 1) * D]
                    S.op("dve", lambda e: e.scalar_tensor_tensor(
                        out=m(1), in0=m(1), scalar=1.0, in1=g(0), op0=ALU.add, op1=ALU.mult),
                        reads=[rRAW, rNG], writes=[rRAW])
                    S.op("dve", lambda e: e.tensor_tensor(out=m(2), in0=m(2), in1=g(1), op=ALU.mult),
                         reads=[rRAW, rNG], writes=[rRAW])
                    S.op("dve", lambda e: e.scalar_tensor_tensor(
                        out=m(4), in0=m(4), scalar=1.0, in1=g(2), op0=ALU.add, op1=ALU.mult),
                        reads=[rRAW, rNG], writes=[rRAW])
                    S.op("dve", lambda e: e.tensor_tensor(out=m(5), in0=m(5), in1=g(3), op=ALU.mult),
                         reads=[rRAW, rNG], writes=[rRAW])
                S.dma("sp", MODD.rearrange("p s i d -> p s (i d)"), RAW[:], reads=[rRAW])
                S.dma("sp", DL[:], dlam[l:l + 1, :].partition_broadcast(128), writes=[rDL])
                S.op("dve", lambda e: e.tensor_tensor(out=TL[:, 0:64], in0=DL[:, 0:64], in1=DL[:, 64:128], op=ALU.mult),
                     reads=[rDL], writes=[rTL])
                S.op("dve", lambda e: e.tensor_tensor(out=TL[:, 64:128], in0=DL[:, 128:192], in1=DL[:, 192:256],
                                                      op=ALU.mult), reads=[rDL], writes=[rTL])
                S.op("dve", lambda e: e.tensor_reduce(out=SC[:, 0:2], in_=TL[:].rearrange("p (a b) -> p a b", a=2),
                                                      axis=AX.X, op=ALU.add), reads=[rTL], writes=[rSC])
                S.op("act", lambda e: e.activation(out=SC[:, 2:4], in_=SC[:, 0:2], func=AF.Exp), reads=[rSC], writes=[rSC])
                S.op("dve", lambda e: e.scalar_tensor_tensor(out=LAM[:, 0:1], in0=SC[:, 3:4], scalar=-lam_init,
                                                             in1=SC[:, 2:3], op0=ALU.add, op1=ALU.subtract),
                     reads=[rSC], writes=[rC])
                S.dma("sp", SUBC[:], dsub[l:l + 1, :].rearrange("o p -> p o"), writes=[rC], allow_slow_non_contiguous=True)
                S.op("dve", lambda e: e.tensor_scalar(out=SUBC[:], in0=SUBC[:], scalar1=1.0 - lam_init, scalar2=None,
                                                      op0=ALU.mult), reads=[rC], writes=[rC])
                S.dma("sp", HNW[:], hnorm[l:l + 1, :].partition_broadcast(128), writes=[rC])
                S.dma("sp", ESK[:], sink[l:l + 1, :].partition_broadcast(128), writes=[rC])
                S.op("act", lambda e: e.activation(out=ESK[:], in_=ESK[:], func=AF.Exp), reads=[rC], writes=[rC])
                S.dma("sp", LB[:].rearrange("p l (r h) -> p l r h", r=2),
                      hlb.rearrange("l r (h p) -> p l r h", p=128), writes=[rLB], allow_slow_non_contiguous=True)
                S.op("act", lambda e: e.activation(out=LE[:], in_=LB[:], func=AF.Exp), reads=[rLB], writes=[rLE])
                S.op("dve", lambda e: e.tensor_copy(out=LS[:], in_=LE[:, 0, :]), reads=[rLE], writes=[rLS])
                for i in range(1, L):
                    S.op("dve", (lambda i=i: lambda e: e.tensor_tensor(out=LS[:], in0=LS[:], in1=LE[:, i, :], op=ALU.add))(),
                         reads=[rLS, rLE], writes=[rLS])
                S.op("dve", lambda e: e.reciprocal(out=LR[:], in_=LS[:]), reads=[rLS], writes=[rLS])
                S.op("pool", lambda e: e.memset(LS[:], 0.0), reads=[rLS], writes=[rLS])
                for i in range(1, l + 1):
                    S.op("dve", (lambda i=i: lambda e: e.tensor_tensor(out=LS[:], in0=LS[:], in1=LE[:, i, :], op=ALU.add))(),
                         reads=[rLS, rLE], writes=[rLS])
                S.op("dve", lambda e: e.tensor_tensor(out=LS[:], in0=LS[:], in1=LR[:], op=ALU.mult), reads=[rLS], writes=[rLS])
                S.op("dve", lambda e: e.tensor_scalar(out=OML[:], in0=LS[:], scalar1=-1.0, scalar2=1.0, op0=ALU.mult,
                                                      op1=ALU.add), reads=[rLS], writes=[rC])
                S.flush()

        def norm_mod_transpose(xt, xres, Gap, SHap, mres, HTt, HTres, j, rings):
            JK, SSr, H1r, Hr, PTr = rings
            jk, jres = JK.next()
            ss, sres = SSr.next()
            h1, h1res = H1r.next()
            hb, hres = Hr.next()
            pt, ptres = PTr.next()
            S.op("act", lambda e: e.activation(out=jk[:], in_=xt, func=AF.Square, accum_out=ss[:, 0:1]),
                 reads=[xres], writes=[jres, sres])
            S.op("act", lambda e: e.activation(out=ss[:, 1:2], in_=ss[:, 0:1], func=AF.Sqrt, scale=1.0 / D, bias=EPS),
                 reads=[sres], writes=[sres])
            S.op("dve", lambda e: e.reciprocal(out=ss[:, 2:3], in_=ss[:, 1:2]), reads=[sres], writes=[sres])
            S.op("dve", lambda e: e.scalar_tensor_tensor(out=h1[:], in0=xt, scalar=ss[:, 2:3], in1=Gap,
                                                         op0=ALU.mult, op1=ALU.mult), reads=[xres, sres, mres], writes=[h1res])
            S.op("pool", lambda e: e.tensor_tensor(out=hb[:], in0=h1[:], in1=SHap, op=ALU.add),
                 reads=[h1res, mres], writes=[hres])
            for k in range(8):
                S.op("pe", (lambda k=k: lambda e: e.transpose(pt[:, k, :], hb[:, k * 128:(k + 1) * 128], IDN[:]))(),
                     reads=[hres, rC], writes=[ptres])
            S.op("act", lambda e: e.activation(out=HTt[:, :, j * 128:(j + 1) * 128], in_=pt[:], func=AF.Copy),
                 reads=[ptres], writes=[HTres])

        def phase_P1(l):
            xsrc = xcat if l == 0 else XS
            with ExitStack() as st:
                XR = Ring(nc, st, "x", 3, [128, D], F32)
                JK = Ring(nc, st, "jk", 1, [128, D], BF16)
                SSr = Ring(nc, st, "ss", 4, [128, 4], F32)
                H1r = Ring(nc, st, "h1", 2, [128, D], F32)
                Hr = Ring(nc, st, "hb", 2, [128, D], BF16)
                PTr = Ring(nc, st, "ptr", 2, [128, 8, 128], BF16, psum=True)
                HTr = Ring(nc, st, "ht", 2, [128, 8, 512], BF16)
                WSr = Ring(nc, st, "ws", 3, [128, 2, 8, 128], BF16)
                WMr = Ring(nc, st, "wm", 2, [128, 8, 512], BF16)
                RPr = Ring(nc, st, "rp", 2, [128, 4, 512], F32)
                PSr = Ring(nc, st, "ps", 4, [128, 512], F32, psum=True)
                T1r = Ring(nc, st, "t1", 2, [128, 512], F32)
                T2r = Ring(nc, st, "t2", 2, [128, 512], F32)
                OBr = Ring(nc, st, "ob", 3, [128, 512], BF16)
                OFr = Ring(nc, st, "of", 3, [128, 512], F32)
                VAr = Ring(nc, st, "va", 2, [128, 4, 129], BF16)
                SVr = Ring(nc, st, "sv", 2, [128, 2, 65], BF16)
                for t_, r_ in zip(VAr.t, VAr.r):
                    S.op("pool", (lambda t_=t_: lambda e: e.memset(t_[:, :, 128:129], 1.0))(), writes=[r_])
                for t_, r_ in zip(SVr.t, SVr.r):
                    S.op("pool", (lambda t_=t_: lambda e: e.memset(t_[:, :, 64:65], 1.0))(), writes=[r_])
                rings = (JK, SSr, H1r, Hr, PTr)
                MODp = sbt(st, "MODp", [128, 2, 2, D], F32)
                rMOD = Res()
                S.dma("sp", MODp[:], MODD[:, :, 0:2, :], writes=[rMOD])
                for (t0, ntok) in supers:
                    mset = 1 if t0 == 0 else 0
                    nt = ntok // 128
                    HTt, HTres = HTr.next()
                    for j in range(nt):
                        xt, xres = XR.next()
                        S.dma("sp", xt[:], xsrc[t0 + j * 128:t0 + (j + 1) * 128, :], writes=[xres])
                        norm_mod_transpose(xt[:], xres, MODp[:, mset, 1, :], MODp[:, mset, 0, :], rMOD, HTt, HTres, j, rings)
                    rp, rpres = RPr.next()
                    S.dma("sp", rp[:, :, 0:ntok], rope[:, :, t0:t0 + ntok].rearrange("a p t -> p a t"), writes=[rpres])
                    for g0 in range(0, NSB, 2):
                        wt, wres = WSr.next()
                        S.dma("sp", wt[:], WS_in[l, g0:g0 + 2].rearrange("b p k c -> p b k c"), writes=[wres])
                        kind = SB_KINDS[g0]
                        pss = []
                        for b in range(2):
                            ps, pres = PSr.next()
                            for k in range(8):
                                S.op("pe", (lambda ps=ps, wt=wt, b=b, k=k: lambda e: e.matmul(
                                    ps[:, 0:ntok], lhsT=wt[:, b, k, :], rhs=HTt[:, k, 0:ntok],
                                    start=(k == 0), stop=(k == 7)))(), reads=[wres, HTres], writes=[pres])
                            pss.append((ps, pres))
                        if kind in ("dq", "dk", "sq", "sk"):
                            (pa, pares), (pb, pbres) = pss
                            ci = 2 if kind in ("dq", "sq") else 0
                            t1, t1res = T1r.next()
                            t2, t2res = T2r.next()
                            ob, obres = OBr.next()
                            S.op("dve", (lambda pa=pa, t1=t1, ci=ci: lambda e: e.tensor_tensor(
                                out=t1[:, 0:ntok], in0=pa[:, 0:ntok], in1=rp[:, ci, 0:ntok], op=ALU.mult))(),
                                reads=[pares, rpres], writes=[t1res])
                            S.op("dve", (lambda pb=pb, t2=t2, ci=ci: lambda e: e.tensor_tensor(
                                out=t2[:, 0:ntok], in0=pb[:, 0:ntok], in1=rp[:, ci + 1, 0:ntok], op=ALU.mult))(),
                                reads=[pbres, rpres], writes=[t2res])
                            S.op("pool", (lambda t1=t1, t2=t2, ob=ob: lambda e: e.tensor_tensor(
                                out=ob[:, 0:ntok], in0=t1[:, 0:ntok], in1=t2[:, 0:ntok], op=ALU.add))(),
                                reads=[t1res, t2res], writes=[obres])
                            pair = g0 // 2
                            if kind == "dq":
                                dst = QT[pair * 128:(pair + 1) * 128, t0:t0 + ntok]
                            elif kind == "dk":
                                dst = KT[(pair - 4) * 128:(pair - 3) * 128, t0:t0 + ntok]
                            elif kind == "sq":
                                dst = SQT[(pair - 8) * 128:(pair - 7) * 128, t0:t0 + ntok]
                            else:
                                dst = SKT[:, t0:t0 + ntok]
                            S.dma("sp", dst, ob[:, 0:ntok], reads=[obres])
                        else:
                            for b, (ps, pres) in enumerate(pss):
                                blk = g0 + b
                                if kind == "gate":
                                    ob, obres = OBr.next()
                                    S.op("act", (lambda ps=ps, ob=ob: lambda e: e.activation(
                                        out=ob[:, 0:ntok], in_=ps[:, 0:ntok], func=AF.Sigmoid))(),
                                        reads=[pres], writes=[obres])
                                    r0 = (blk - 38) * 128
                                    S.dma("sp", GT[r0:r0 + 128, t0:t0 + ntok], ob[:, 0:ntok], reads=[obres])
                                else:
                                    of, ofres = OFr.next()
                                    if kind == "hq":
                                        S.op("act", (lambda ps=ps, of=of: lambda e: e.activation(
                                            out=of[:, 0:ntok], in_=ps[:, 0:ntok], func=AF.Silu))(),
                                            reads=[pres], writes=[ofres])
                                        r0 = (blk - 26) * 128
                                        dst = HQT[r0:r0 + 128, t0:t0 + ntok]
                                    else:
                                        S.op("act", (lambda ps=ps, of=of: lambda e: e.activation(
                                            out=of[:, 0:ntok], in_=ps[:, 0:ntok], func=AF.Sigmoid, scale=-1.0))(),
                                            reads=[pres], writes=[ofres])
                                        if kind == "zf":
                                            r0 = (blk - 30) * 128
                                            dst = SGT[0, r0:r0 + 128, t0:t0 + ntok]
                                        else:
                                            r0 = (blk - 34) * 128
                                            dst = SGT[1, r0:r0 + 128, t0:t0 + ntok]
                                    S.dma("sp", dst, of[:, 0:ntok], reads=[ofres])
                    for cb, (c0, ncol) in enumerate(((0, 512), (512, 512), (1024, 512), (1536, 128))):
                        wm, wmres = WMr.next()
                        S.dma("sp", wm[:, :, 0:ncol], WM_in[l, :, :, c0:c0 + ncol], writes=[wmres])
                        for j in range(nt):
                            ps, pres = PSr.next()
                            for k in range(8):
                                S.op("pe", (lambda ps=ps, wm=wm, j=j, k=k, ncol=ncol: lambda e: e.matmul(
                                    ps[:, 0:ncol], lhsT=HTt[:, k, j * 128:(j + 1) * 128], rhs=wm[:, k, 0:ncol],
                                    start=(k == 0), stop=(k == 7)))(), reads=[wmres, HTres], writes=[pres])
                            tk = slice(t0 + j * 128, t0 + (j + 1) * 128)
                            if cb == 0:
                                va, vares = VAr.next()
                                S.op("act", (lambda ps=ps, va=va: lambda e: e.activation(
                                    out=va[:, :, 0:128], in_=ps[:].rearrange("p (h c) -> p h c", h=4), func=AF.Copy))(),
                                    reads=[pres], writes=[vares])
                                S.dma("sp", DV[tk], va[:], reads=[vares])
                            elif cb == 1:
                                ob, obres = OBr.next()
                                S.op("act", (lambda ps=ps, ob=ob: lambda e: e.activation(
                                    out=ob[:], in_=ps[:], func=AF.Copy))(), reads=[pres], writes=[obres])
                                S.dma("sp", HV[tk, :], ob[:], reads=[obres])
                            elif cb == 2:
                                of, ofres = OFr.next()
                                S.op("act", (lambda ps=ps, of=of: lambda e: e.activation(
                                    out=of[:], in_=ps[:], func=AF.Silu))(), reads=[pres], writes=[ofres])
                                S.dma("sp", HG[tk, :], of[:], reads=[ofres])
                            else:
                                sv, svres = SVr.next()
                                S.op("act", (lambda ps=ps, sv=sv: lambda e: e.activation(
                                    out=sv[:, :, 0:64], in_=ps[:, 0:128].rearrange("p (h c) -> p h c", h=2),
                                    func=AF.Copy))(), reads=[pres], writes=[svres])
                                S.dma("sp", SV[tk], sv[:], reads=[svres])
                S.flush()

        def phase_P2(l):
            need_ctx = l < L - 1
            with ExitStack() as st:
                KTh = sbt(st, "KTh", [128, T], BF16)
                Vh = sbt(st, "Vh", [128, NT, 128], BF16)
                ONES = sbt(st, "ONES", [128, 128], F32)
                rK = Res(); rV = Res(); rO = Res()
                S.op("pool", lambda e: e.memset(ONES[:], 1.0), writes=[rO])
                QBr = Ring(nc, st, "qb", 2, [128, 512], BF16)
                STr = [Ring(nc, st, "st%d" % c, 2, [128, 512], F32, psum=True) for c in range(2)]
                PTr = [Ring(nc, st, "pt%d" % c, 3, [128, 512], BF16) for c in range(2)]
                OTr = [Ring(nc, st, "ot%d" % c, 1, [128, 512], F32, psum=True) for c in range(2)]
                RSr = Ring(nc, st, "rsp", 2, [128, 512], F32, psum=True)
                RAr = [Ring(nc, st, "ra%d" % c, 2, [128, 512], F32) for c in range(2)]
                RRr = [Ring(nc, st, "rr%d" % c, 1, [128, 512], F32) for c in range(2)]
                O0r = Ring(nc, st, "o0", 1, [128, 512], F32)
                T1r = Ring(nc, st, "t1p", 1, [128, 512], F32)
                OOr = Ring(nc, st, "oo", 2, [128, 512], F32)
                SQr = Ring(nc, st, "sqq", 1, [128, 512], F32)
                SDr = Ring(nc, st, "sd", 1, [128, 512], F32)
                YAr = Ring(nc, st, "ya", 2, [128, 512], BF16)
                qblocks = ([(0, CTX, [0, 1])] if need_ctx else []) + \
                          [(CTX + 512 * i, 512, list(range(NT))) for i in range(SEQ // 512)]
                acc_eng = ("pool", "dve")
                for h in range(4):
                    S.dma("sp", KTh[:], KT[h * 128:(h + 1) * 128, :], writes=[rK])
                    S.dma("sp", Vh[:], DV[:, h, 0:128].rearrange("(k p) c -> p k c", p=128), writes=[rV])
                    for (q0, nq, kts) in qblocks:
                        qb, qres = QBr.next()
                        S.dma("sp", qb[:, 0:nq], QT[h * 128:(h + 1) * 128, q0:q0 + nq], writes=[qres])
                        ots = [OTr[c].next() for c in range(2)]
                        ras = [RAr[c].next() for c in range(2)]
                        for ki, kt in enumerate(kts):
                            pts = []
                            for c in range(2):
                                stt, stres = STr[c].next()
                                S.op("pe", lambda e: e.matmul(
                                    stt[:, 0:nq], lhsT=KTh[c * 64:(c + 1) * 64, kt * 128:(kt + 1) * 128],
                                    rhs=qb[c * 64:(c + 1) * 64, 0:nq], start=True, stop=True),
                                    reads=[rK, qres], writes=[stres])
                                pt, ptres = PTr[c].next()
                                S.op("act", lambda e: e.activation(out=pt[:, 0:nq], in_=stt[:, 0:nq], func=AF.Exp),
                                     reads=[stres], writes=[ptres])
                                pts.append((pt, ptres))
                            for c in range(2):
                                pt, ptres = pts[c]
                                ot, otres = ots[c]
                                S.op("pe", lambda e: e.matmul(ot[:, 0:nq], lhsT=Vh[:, kt, :], rhs=pt[:, 0:nq],
                                                              start=(ki == 0), stop=(ki == len(kts) - 1)),
                                     reads=[ptres, rV], writes=[otres])
                                ra, rares = ras[c]
                                if ki == 0:
                                    S.op(acc_eng[c], lambda e: e.tensor_copy(out=ra[:, 0:nq], in_=pt[:, 0:nq]),
                                         reads=[ptres], writes=[rares])
                                else:
                                    S.op(acc_eng[c], lambda e: e.tensor_tensor(out=ra[:, 0:nq], in0=ra[:, 0:nq], in1=pt[:, 0:nq],
                                                                               op=ALU.add), reads=[ptres, rares], writes=[rares])
                        rrs = []
                        for c in range(2):
                            ra, rares = ras[c]
                            rs, rsres = RSr.next()
                            S.op("pe", lambda e: e.matmul(rs[:, 0:nq], lhsT=ONES[:], rhs=ra[:, 0:nq], start=True, stop=True),
                                 reads=[rO, rares], writes=[rsres])
                            rr, rrres = RRr[c].next()
                            S.op("dve", lambda e: e.reciprocal(out=rr[:, 0:nq], in_=rs[:, 0:nq]), reads=[rsres], writes=[rrres])
                            rrs.append((rr, rrres))
                        o0, o0res = O0r.next()
                        t1, t1res = T1r.next()
                        oo, oores = OOr.next()
                        sq, sqres = SQr.next()
                        sd, sdres = SDr.next()
                        ya, yares = YAr.next()
                        S.op("dve", lambda e: e.tensor_tensor(out=o0[:, 0:nq], in0=ots[0][0][:, 0:nq], in1=rrs[0][0][:, 0:nq],
                                                              op=ALU.mult), reads=[ots[0][1], rrs[0][1]], writes=[o0res])
                        S.op("dve", lambda e: e.tensor_tensor(out=t1[:, 0:nq], in0=ots[1][0][:, 0:nq], in1=rrs[1][0][:, 0:nq],
                                                              op=ALU.mult), reads=[ots[1][1], rrs[1][1]], writes=[t1res])
                        S.op("dve", lambda e: e.scalar_tensor_tensor(out=oo[:, 0:nq], in0=t1[:, 0:nq], scalar=LAM[:, 0:1],
                                                                      in1=o0[:, 0:nq], op0=ALU.mult, op1=ALU.add),
                             reads=[t1res, o0res, rC], writes=[oores])
                        S.op("pool", lambda e: e.tensor_tensor(out=sq[:, 0:nq], in0=oo[:, 0:nq], in1=oo[:, 0:nq], op=ALU.mult),
                             reads=[oores], writes=[sqres])
                        rs, rsres = RSr.next()
                        S.op("pe", lambda e: e.matmul(rs[:, 0:nq], lhsT=ONES[:], rhs=sq[:, 0:nq], start=True, stop=True),
                             reads=[rO, sqres], writes=[rsres])
                        S.op("act", lambda e: e.activation(out=sd[:, 0:nq], in_=rs[:, 0:nq], func=AF.Sqrt, scale=1.0 / 128, bias=EPS),
                             reads=[rsres], writes=[sdres])
                        S.op("dve", lambda e: e.reciprocal(out=sd[:, 0:nq], in_=sd[:, 0:nq]), reads=[sdres], writes=[sdres])
                        S.op("dve", lambda e: e.scalar_tensor_tensor(out=ya[:, 0:nq], in0=oo[:, 0:nq], scalar=SUBC[:, 0:1],
                                                                     in1=sd[:, 0:nq], op0=ALU.mult, op1=ALU.mult),
                             reads=[oores, sdres, rC], writes=[yares])
                        S.dma("sp", YAT[h * 128:(h + 1) * 128, q0:q0 + nq], ya[:, 0:nq], reads=[yares])
                S.flush()

        def phase_P4(l):
            need_ctx = l < L - 1
            with ExitStack() as st:
                SKg = sbt(st, "SKg", [64, T], BF16)
                SVg = sbt(st, "SVg", [128, NT, 65], BF16)
                rK = Res(); rV = Res()
                SQr = Ring(nc, st, "sq", 3, [64, 4, 128], BF16)
                STr = Ring(nc, st, "sst", 3, [128, 512], F32, psum=True)
                PTr = Ring(nc, st, "spt", 4, [128, 4, 128], BF16)
                Or = Ring(nc, st, "so", 2, [128, 4, 65], F32, psum=True)
                DNr = Ring(nc, st, "dn", 3, [128, 8], F32)
                YCr = Ring(nc, st, "yc", 3, [128, 4, 64], BF16)
                for g in range(2):
                    S.dma("sp", SKg[:], SKT[g * 64:(g + 1) * 64, :], writes=[rK])
                    S.dma("sp", SVg[:], SV[:, g, :].rearrange("(k p) c -> p k c", p=128), writes=[rV])
                    for qt in range(0 if need_ctx else 2, NT):
                        if qt < 2:
                            keys = [(0, None), (1, None)]
                        else:
                            keys = [(0, None), (1, None)]
                            if qt - 1 >= 2:
                                keys.append((qt - 1, 0))
                            keys.append((qt, None))
                            if qt + 1 < NT:
                                keys.append((qt + 1, 1))
                        sq, sqres = SQr.next()
                        S.dma("sp", sq[:], SQT[g * 256:(g + 1) * 256, qt * 128:(qt + 1) * 128].rearrange(
                            "(i d) t -> d i t", d=64), writes=[sqres])
                        o, ores = Or.next()
                        for ki, (kt, mk) in enumerate(keys):
                            stt, stres = STr.next()
                            pt, ptres = PTr.next()
                            S.op("pe", (lambda stt=stt, kt=kt, sq=sq: lambda e: e.matmul(
                                stt[:], lhsT=SKg[:, kt * 128:(kt + 1) * 128], rhs=sq[:].rearrange("d i t -> d (i t)"),
                                start=True, stop=True))(), reads=[rK, sqres], writes=[stres])
                            S.op("act", (lambda stt=stt, pt=pt: lambda e: e.activation(
                                out=pt[:].rearrange("p i t -> p (i t)"), in_=stt[:], func=AF.Exp))(),
                                reads=[stres], writes=[ptres])
                            if mk is not None:
                                S.op("pool", (lambda pt=pt, mk=mk: lambda e: e.tensor_tensor(
                                    out=pt[:], in0=pt[:], in1=MSK[:, mk:mk + 1, :].broadcast_to([128, 4, 128]),
                                    op=ALU.mult))(), reads=[ptres, rC], writes=[ptres])
                            for i in range(4):
                                S.op("pe", (lambda o=o, pt=pt, i=i, kt=kt: lambda e: e.matmul(
                                    o[:, i, :], lhsT=pt[:, i, :], rhs=SVg[:, kt, :],
                                    start=(ki == 0 and i == 0), stop=(ki == len(keys) - 1), skip_group_check=True))(),
                                    reads=[ptres, rV], writes=[ores])
                        dn, dnres = DNr.next()
                        yc, ycres = YCr.next()
                        S.op("dve", (lambda o=o, dn=dn, g=g: lambda e: e.tensor_tensor(
                            out=dn[:, 0:4], in0=o[:, :, 64], in1=ESK[:, g * 4:(g + 1) * 4], op=ALU.add))(),
                            reads=[ores, rC], writes=[dnres])
                        S.op("dve", (lambda dn=dn: lambda e: e.reciprocal(out=dn[:, 4:8], in_=dn[:, 0:4]))(),
                             reads=[dnres], writes=[dnres])
                        S.op("dve", (lambda o=o, dn=dn, yc=yc: lambda e: e.tensor_tensor(
                            out=yc[:], in0=o[:, :, 0:64], in1=dn[:, 4:8].unsqueeze(2).broadcast_to([128, 4, 64]),
                            op=ALU.mult))(), reads=[ores, dnres], writes=[ycres])
                        S.dma("sp", YC[qt * 128:(qt + 1) * 128, g * 256:(g + 1) * 256],
                              yc[:].rearrange("p i d -> p (i d)"), reads=[ycres])
                S.flush()

        def phase_P3(l):
            need_ctx = l < L - 1
            SEGN = 32
            blocks = [(0, CTX // 64)] + [(CTX + 512 * i, 8) for i in range(SEQ // 512)]
            order = {0: [(b, list(range(b[1]))) for b in blocks],
                     1: [(blocks[0], list(range(blocks[0][1]))[::-1])] +
                        [(b, list(range(b[1]))[::-1]) for b in blocks[1:][::-1]]}
            pos_of = {0: {}, 1: {}}
            for r in range(2):
                p = 0
                for (t0, nch), chs in order[r]:
                    for ci in chs:
                        pos_of[r][t0 // 64 + ci] = p
                        p += 1
            with ExitStack() as st:
                RST = sbt(st, "RST", [128, 512], F32)
                rRST = Res()
                S.dma("sp", RST[:], reset_d, writes=[rRST])
                QET = [sbt(st, "QET%d" % r, [128, T], BF16) for r in range(2)]
                rQET = [[Res() for _ in range(len(blocks))] for r in range(2)]
                ATA = [sbt(st, "ATA%d" % r, [32, NCH, 64], BF16) for r in range(2)]
                ATB = [sbt(st, "ATB%d" % r, [32, NCH, 64], BF16) for r in range(2)]
                rATB0 = Res()
                for r in range(2):
                    S.op("pool", (lambda r=r: lambda e: e.memset(ATB[r][:], 0.0))(), writes=[rATB0])
                rAT = [[Res() for _ in range(len(blocks))] for r in range(2)]
                SGr = Ring(nc, st, "sg", 2, [128, 512], F32)
                HQr = Ring(nc, st, "hq", 2, [128, 512], F32)
                KKr = Ring(nc, st, "kk", 1, [128, 512], F32)
                LFr = Ring(nc, st, "lf", 1, [128, 512], F32)
                BBr = Ring(nc, st, "bb", 1, [128, 512], F32)
                CCr = Ring(nc, st, "cc", 1, [128, 512], F32)
                EPr = Ring(nc, st, "ep", 2, [128, 512], F32)
                ENr = Ring(nc, st, "en", 2, [128, 512], F32)
                KEr = Ring(nc, st, "ke", 2, [128, 512], BF16)
                AXr = Ring(nc, st, "ax", 2, [128, 3, 8], F32)
                EXr = Ring(nc, st, "ex", 2, [128, 3, 8], F32)
                VBr = Ring(nc, st, "vb", 3, [64, 8, 128], BF16)
                VHr = Ring(nc, st, "vh", 2, [32, 16, 128], BF16)
                KCr = Ring(nc, st, "kc", 3, [64, 128], BF16)
                PKr = Ring(nc, st, "pk", 2, [64, 128], BF16, psum=True)
                PUr = Ring(nc, st, "pu", 2, [128, 128], F32, psum=True)
                PAr = Ring(nc, st, "pa", 2, [32, 128], F32, psum=True)
                POr = Ring(nc, st, "po", 2, [64, 128], F32, psum=True)
                EUr = Ring(nc, st, "eu", 1, [128, SEGN, 128], F32)
                SCr = Ring(nc, st, "scs", 2, [128, 2, SEGN], F32)
                SPr = Ring(nc, st, "sps", 1, [128, SEGN, 128], BF16)
                CARRY = [sbt(st, "CAR%d" % i, [128, 128], F32) for i in range(2)]
                rCAR = [Res(), Res()]
                SFr = Ring(nc, st, "sf", 4, [128, 8, 128], BF16)
                GGr = Ring(nc, st, "gg", 2, [64, 8, 128], F32)
                RSr = Ring(nc, st, "rs", 4, [64, 4], F32)
                JKr = Ring(nc, st, "jk3", 2, [64, 128], BF16)
                Y1r = Ring(nc, st, "y1", 2, [64, 128], F32)
                YBr = Ring(nc, st, "yb", 2, [64, 8, 128], BF16)
                for h in range(4):
                    for r in range(2):
                        car = 0
                        S.op("pool", (lambda car=car: lambda e: e.memset(CARRY[car][:], 0.0))(), writes=[rCAR[car]])
                        seg = []
                        eu, eures = EUr.next()
                        sc, scres = SCr.next()
                        nslot = 0
                        total = sum(len(chs) for _, chs in order[r])
                        done = 0
                        for bi_, ((t0, nch), chs) in enumerate(order[r]):
                            bidx = blocks.index((t0, nch))
                            nb = nch * 64
                            sg, sgres = SGr.next()
                            hq, hqres = HQr.next()
                            kk, kkres = KKr.next()
                            lf, lfres = LFr.next()
                            bb, bbres = BBr.next()
                            cc, ccres = CCr.next()
                            ep, epres = EPr.next()
                            en, enres = ENr.next()
                            ke, keres = KEr.next()
                            ax, axres = AXr.next()
                            ex, exres = EXr.next()
                            vb, vbres = VBr.next()
                            S.dma("sp", sg[:, 0:nb], SGT[r, h * 128:(h + 1) * 128, t0:t0 + nb], writes=[sgres])
                            S.dma("sp", hq[:, 0:nb], HQT[h * 128:(h + 1) * 128, t0:t0 + nb], writes=[hqres])
                            S.dma("sp", vb[:, 0:nch, :], HV[t0:t0 + nb, h * 128:(h + 1) * 128].rearrange(
                                "(c s) d -> s c d", s=64), writes=[vbres])
                            oc = r * 4 + h
                            S.op("dve", (lambda kk=kk, sg=sg, nb=nb, oc=oc: lambda e: e.tensor_scalar(
                                out=kk[:, 0:nb], in0=sg[:, 0:nb], scalar1=OML[:, oc:oc + 1], scalar2=None, op0=ALU.mult))(),
                                reads=[sgres, rC], writes=[kkres])
                            S.op("act", (lambda lf=lf, kk=kk, nb=nb: lambda e: e.activation(
                                out=lf[:, 0:nb], in_=kk[:, 0:nb], func=AF.Ln, scale=-1.0, bias=1.0))(),
                                reads=[kkres], writes=[lfres])
                            S.op("dve", (lambda bb=bb, lf=lf, nb=nb: lambda e: e.tensor_tensor_scan(
                                out=bb[:, 0:nb], data0=RST[:, 0:nb], data1=lf[:, 0:nb], initial=0.0,
                                op0=ALU.mult, op1=ALU.add))(), reads=[lfres, rRST], writes=[bbres])
                            v3 = lambda t, nch=nch: t[:, 0:nch * 64].rearrange("p (c t) -> p c t", t=64)
                            if r == 0:
                                S.op("dve", (lambda cc=cc, bb=bb, nch=nch, v3=v3: lambda e: e.tensor_tensor(
                                    out=v3(cc), in0=v3(bb), in1=v3(bb)[:, :, 31:32].broadcast_to([128, nch, 64]),
                                    op=ALU.subtract))(), reads=[bbres], writes=[ccres])
                                S.op("pool", (lambda ax=ax, bb=bb, nch=nch, v3=v3: lambda e: e.tensor_copy(
                                    out=ax[:, 0, 0:nch], in_=v3(bb)[:, :, 63]))(), reads=[bbres], writes=[axres])
                                S.op("pool", (lambda ax=ax, bb=bb, nch=nch, v3=v3: lambda e: e.tensor_copy(
                                    out=ax[:, 1, 0:nch], in_=v3(bb)[:, :, 31]))(), reads=[bbres], writes=[axres])
                                S.op("pool", (lambda ax=ax, nch=nch: lambda e: e.tensor_tensor(
                                    out=ax[:, 2, 0:nch], in0=ax[:, 0, 0:nch], in1=ax[:, 1, 0:nch], op=ALU.subtract))(),
                                    reads=[axres], writes=[axres])
                            else:
                                S.op("dve", (lambda bb=bb, lf=lf, kk=kk, nb=nb: lambda e: e.tensor_tensor(
                                    out=lf[:, 0:nb], in0=bb[:, 0:nb], in1=lf[:, 0:nb], op=ALU.subtract))(),
                                    reads=[bbres, lfres], writes=[lfres])
                                S.op("dve", (lambda cc=cc, lf=lf, nch=nch, v3=v3: lambda e: e.tensor_tensor(
                                    out=v3(cc), in0=v3(lf), in1=v3(lf)[:, :, 31:32].broadcast_to([128, nch, 64]),
                                    op=ALU.subtract))(), reads=[lfres], writes=[ccres])
                                S.op("pool", (lambda ax=ax, bb=bb, nch=nch, v3=v3: lambda e: e.tensor_copy(
                                    out=ax[:, 0, 0:nch], in_=v3(bb)[:, :, 63]))(), reads=[bbres], writes=[axres])
                                S.op("pool", (lambda ax=ax, lf=lf, nch=nch, v3=v3: lambda e: e.tensor_copy(
                                    out=ax[:, 2, 0:nch], in_=v3(lf)[:, :, 31]))(), reads=[lfres], writes=[axres])
                                S.op("pool", (lambda ax=ax, nch=nch: lambda e: e.tensor_tensor(
                                    out=ax[:, 1, 0:nch], in0=ax[:, 0, 0:nch], in1=ax[:, 2, 0:nch], op=ALU.subtract))(),
                                    reads=[axres], writes=[axres])
                            S.op("act", (lambda ex=ex, ax=ax: lambda e: e.activation(out=ex[:], in_=ax[:], func=AF.Exp))(),
                                 reads=[axres], writes=[exres])
                            S.op("act", (lambda ep=ep, cc=cc, nb=nb: lambda e: e.activation(
                                out=ep[:, 0:nb], in_=cc[:, 0:nb], func=AF.Exp))(), reads=[ccres], writes=[epres])
                            S.op("act", (lambda en=en, cc=cc, nb=nb: lambda e: e.activation(
                                out=en[:, 0:nb], in_=cc[:, 0:nb], func=AF.Exp, scale=-1.0))(), reads=[ccres], writes=[enres])
                            qmul, qmr = (ep, epres) if r == 0 else (en, enres)
                            kmul, kmr = (en, enres) if r == 0 else (ep, epres)
                            S.op("pool", (lambda qmul=qmul, hq=hq, nb=nb, t0=t0, r=r: lambda e: e.tensor_tensor(
                                out=QET[r][:, t0:t0 + nb], in0=hq[:, 0:nb], in1=qmul[:, 0:nb], op=ALU.mult))(),
                                reads=[hqres, qmr], writes=[rQET[r][bidx]])
                            S.op("dve", (lambda kmul=kmul, kk=kk, ke=ke, nb=nb: lambda e: e.tensor_tensor(
                                out=ke[:, 0:nb], in0=kk[:, 0:nb], in1=kmul[:, 0:nb], op=ALU.mult))(),
                                reads=[kkres, kmr], writes=[keres])
                            for ci in chs:
                                cg = t0 // 64 + ci
                                cs = slice(ci * 64, (ci + 1) * 64)
                                gs = slice(t0 + ci * 64, t0 + (ci + 1) * 64)
                                pa, pares = PAr.next()
                                lo = slice(ci * 64, ci * 64 + 32)
                                hi = slice(ci * 64 + 32, ci * 64 + 64)
                                glo = slice(t0 + ci * 64, t0 + ci * 64 + 32)
                                ghi = slice(t0 + ci * 64 + 32, t0 + ci * 64 + 64)
                                sA, sB, gB = (lo, hi, ghi) if r == 0 else (hi, lo, glo)
                                S.op("pe", lambda e: e.matmul(pa[:, 0:64], lhsT=ke[:, sA], rhs=QET[r][:, gs],
                                                              start=True, stop=True, skip_group_check=True),
                                     reads=[keres, rQET[r][bidx]], writes=[pares])
                                S.op("pe", lambda e: e.matmul(pa[:, 64:96], lhsT=ke[:, sB], rhs=QET[r][:, gB],
                                                              start=False, stop=True, skip_group_check=True),
                                     reads=[keres, rQET[r][bidx]], writes=[pares])
                                if r == 0:
                                    mA = MSK[0:32, 2, 0:64]
                                    mB = MSK[0:32, 2, 0:32]
                                    oB = ATB[r][:, cg, 32:64]
                                else:
                                    mA = MSK[0:32, 4, 0:64]
                                    mB = MSK[0:32, 3, 0:32]
                                    oB = ATB[r][:, cg, 0:32]
                                S.op("dve", lambda e: e.tensor_tensor(out=ATA[r][:, cg, :], in0=pa[:, 0:64], in1=mA, op=ALU.mult),
                                     reads=[pares, rC], writes=[rAT[r][bidx]])
                                S.op("dve", lambda e: e.tensor_tensor(out=oB, in0=pa[:, 64:96], in1=mB, op=ALU.mult),
                                     reads=[pares, rC, rATB0], writes=[rAT[r][bidx]])
                                pk, pkres = PKr.next()
                                kc, kcres = KCr.next()
                                S.op("pe", (lambda pk=pk, ke=ke, cs=cs: lambda e: e.transpose(pk[:], ke[:, cs], IDN[:]))(),
                                     reads=[keres, rC], writes=[pkres])
                                S.op("act", (lambda pk=pk, kc=kc: lambda e: e.activation(out=kc[:], in_=pk[:], func=AF.Copy))(),
                                     reads=[pkres], writes=[kcres])
                                pu, pures = PUr.next()
                                S.op("pe", (lambda pu=pu, kc=kc, vb=vb, ci=ci: lambda e: e.matmul(
                                    pu[:], lhsT=kc[:], rhs=vb[:, ci, :], start=True, stop=True))(),
                                    reads=[kcres, vbres], writes=[pures])
                                sl = nslot
                                S.op("dve", (lambda pu=pu, eu=eu, ex=ex, sl=sl, ci=ci: lambda e: e.tensor_scalar(
                                    out=eu[:, sl, :], in0=pu[:], scalar1=ex[:, 2, ci:ci + 1], scalar2=None, op0=ALU.mult))(),
                                    reads=[pures, exres], writes=[eures])
                                S.op("pool", (lambda sc=sc, ex=ex, sl=sl, ci=ci: lambda e: e.tensor_copy(
                                    out=sc[:, :, sl], in_=ex[:, 0:2, ci]))(), reads=[exres], writes=[scres])
                                nslot += 1
                                done += 1
                                if nslot == SEGN or done == total:
                                    n = nslot
                                    cin, cinres = CARRY[car], rCAR[car]
                                    cout, coutres = CARRY[1 - car], rCAR[1 - car]
                                    sp_, spres = SPr.next()
                                    S.op("pool", (lambda sp_=sp_, cin=cin, sc=sc: lambda e: e.tensor_scalar(
                                        out=sp_[:, 0, :], in0=cin[:], scalar1=sc[:, 1, 0:1], scalar2=None, op0=ALU.mult))(),
                                        reads=[cinres, scres], writes=[spres])
                                    for dv in range(128):
                                        S.op("dve", (lambda eu=eu, sc=sc, cin=cin, dv=dv, n=n: lambda e: e.tensor_tensor_scan(
                                            out=eu[:, 0:n, dv], data0=sc[:, 0, 0:n], data1=eu[:, 0:n, dv],
                                            initial=cin[:, dv:dv + 1], op0=ALU.mult, op1=ALU.add))(),
                                            reads=([eures, scres, cinres] if dv == 0 else []), writes=[eures])
                                    S.op("pool", (lambda cout=cout, eu=eu, n=n: lambda e: e.tensor_copy(
                                        out=cout[:], in_=eu[:, n - 1, :]))(), reads=[eures], writes=[coutres])
                                    if n > 1:
                                        S.op("pool", (lambda sp_=sp_, eu=eu, sc=sc, n=n: lambda e: e.tensor_tensor(
                                            out=sp_[:, 1:n, :], in0=eu[:, 0:n - 1, :],
                                            in1=sc[:, 1, 1:n].unsqueeze(2).broadcast_to([128, n - 1, 128]), op=ALU.mult))(),
                                            reads=[eures, scres], writes=[spres])
                                    p0 = done - n
                                    S.dma("sp", SPD[r, p0:p0 + n].rearrange("c k v -> k c v"), sp_[:, 0:n, :], reads=[spres])
                                    car = 1 - car
                                    nslot = 0
                                    eu, eures = EUr.next()
                                    sc, scres = SCr.next()
                    S.flush()
                    if dbg:
                        for r in range(2):
                            S.dma("sp", DBGQ[h, r], QET[r][:], reads=[])
                            pass
                    for bidx, (t0, nch) in enumerate(blocks):
                        if t0 == 0 and not need_ctx:
                            continue
                        nb = nch * 64
                        vb, vbres = VBr.next()
                        gg, ggres = GGr.next()
                        yb, ybres = YBr.next()
                        vh, vhres = VHr.next()
                        S.dma("sp", vh[:, 0:2 * nch, :], HV[t0:t0 + nb, h * 128:(h + 1) * 128].rearrange(
                            "(c s) d -> s c d", s=32), writes=[vhres])
                        S.dma("sp", gg[:, 0:nch, :], HG[t0:t0 + nb, h * 128:(h + 1) * 128].rearrange(
                            "(c s) d -> s c d", s=64), writes=[ggres])
                        sfs = []
                        for r in range(2):
                            sf, sfres = SFr.next()
                            for ci in range(nch):
                                p = pos_of[r][t0 // 64 + ci]
                                S.dma("sp", sf[:, ci, :], SPD[r, p], writes=[sfres])
                            sfs.append((sf, sfres))
                        for ci in range(nch):
                            cg = t0 // 64 + ci
                            gs = slice(t0 + ci * 64, t0 + (ci + 1) * 64)
                            po, pores = POr.next()
                            for r in range(2):
                                sf, sfres = sfs[r]
                                S.op("pe", (lambda po=po, sf=sf, ci=ci, gs=gs, r=r: lambda e: e.matmul(
                                    po[:], lhsT=QET[r][:, gs], rhs=sf[:, ci, :], start=(r == 0), stop=False))(),
                                    reads=[rQET[r][bidx], sfres], writes=[pores])
                                hA, hB = (0, 1) if r == 0 else (1, 0)
                                S.op("pe", lambda e: e.matmul(po[:], lhsT=ATA[r][:, cg, :], rhs=vh[:, 2 * ci + hA, :],
                                                              start=False, stop=False),
                                     reads=[rAT[r][bidx], vhres], writes=[pores])
                                S.op("pe", lambda e: e.matmul(po[:], lhsT=ATB[r][:, cg, :], rhs=vh[:, 2 * ci + hB, :],
                                                              start=False, stop=(r == 1)),
                                     reads=[rAT[r][bidx], vhres], writes=[pores])
                            rs, rsres = RSr.next()
                            jk, jkres = JKr.next()
                            y1, y1res = Y1r.next()
                            S.op("act", (lambda po=po, jk=jk, rs=rs: lambda e: e.activation(
                                out=jk[:], in_=po[:], func=AF.Square, accum_out=rs[:, 0:1]))(),
                                reads=[pores], writes=[jkres, rsres])
                            S.op("act", (lambda rs=rs: lambda e: e.activation(
                                out=rs[:, 1:2], in_=rs[:, 0:1], func=AF.Sqrt, scale=1.0 / 128, bias=EPS))(),
                                reads=[rsres], writes=[rsres])
                            S.op("dve", (lambda rs=rs: lambda e: e.reciprocal(out=rs[:, 2:3], in_=rs[:, 1:2]))(),
                                 reads=[rsres], writes=[rsres])
                            S.op("dve", (lambda po=po, rs=rs, y1=y1: lambda e: e.scalar_tensor_tensor(
                                out=y1[:], in0=po[:], scalar=rs[:, 2:3], in1=HNW[0:64, :], op0=ALU.mult, op1=ALU.mult))(),
                                reads=[pores, rsres, rC], writes=[y1res])
                            S.op("pool", (lambda y1=y1, gg=gg, yb=yb, ci=ci: lambda e: e.tensor_tensor(
                                out=yb[:, ci, :], in0=y1[:], in1=gg[:, ci, :], op=ALU.mult))(),
                                reads=[y1res, ggres], writes=[ybres])
                        S.dma("sp", YB[t0:t0 + nb, h * 128:(h + 1) * 128].rearrange("(c s) d -> s c d", s=64),
                              yb[:, 0:nch, :], reads=[ybres])
                    S.flush()

        def phase_P5a(l):
            need_ctx = l < L - 1
            xsrc = xcat if l == 0 else XS
            with ExitStack() as st:
                WO = sbt(st, "WO", [128, 8, D], BF16)
                rWO = Res()
                S.dma("sp", WO[:], WM_out[l], writes=[rWO])
                YIr = Ring(nc, st, "yi", 3, [128, 512], BF16)
                PTr = Ring(nc, st, "ptb", 2, [128, 8, 128], BF16, psum=True)
                YTr = [Ring(nc, st, "yt%d" % b, 2, [128, 4, 512], BF16) for b in range(3)]
                WBr = Ring(nc, st, "wbr", 3, [128, 3, 4, 128], BF16)
                GTr = Ring(nc, st, "gt", 3, [128, 3, 512], BF16)
                PSr = Ring(nc, st, "psb", 4, [128, 512], F32, psum=True)
                M0r = Ring(nc, st, "m0", 2, [128, 512], F32)
                M1r = Ring(nc, st, "m1", 2, [128, 512], F32)
                M2r = Ring(nc, st, "m2", 2, [128, 512], F32)
                MTr = Ring(nc, st, "mt", 2, [128, 8, 512], BF16)
                XR = Ring(nc, st, "x5", 2, [128, D], F32)
                TTr = Ring(nc, st, "tt", 2, [128, D], F32)
                X1r = Ring(nc, st, "x1", 2, [128, D], F32)
                JK = Ring(nc, st, "jk5", 1, [128, D], BF16)
                SSr = Ring(nc, st, "ss5", 4, [128, 4], F32)
                S2r = Ring(nc, st, "s25", 4, [128, 4], F32)
                H1r = Ring(nc, st, "h15", 2, [128, D], F32)
                Hr = Ring(nc, st, "hb5", 2, [128, D], BF16)
                HTr = Ring(nc, st, "ht5", 2, [128, 8, 512], BF16)
                rings = (JK, SSr, H1r, Hr, PTr)
                MODp = sbt(st, "MODp5", [128, 2, 3, D], F32)
                rMOD = Res()
                S.dma("sp", MODp[:], MODD[:, :, 2:5, :], writes=[rMOD])
                for (t0, ntok) in supers:
                    if t0 == 0 and not need_ctx:
                        continue
                    mset = 1 if t0 == 0 else 0
                    nt = ntok // 128
                    yts = [YTr[b].next() for b in range(3)]
                    S.dma("sp", yts[0][0][:, :, 0:ntok], YAT[:, t0:t0 + ntok].rearrange("(k p) t -> p k t", p=128),
                          writes=[yts[0][1]])
                    for b, ysrc in ((1, YB), (2, YC)):
                        yt, ytres = yts[b]
                        for j in range(nt):
                            yi, yires = YIr.next()
                            S.dma("sp", yi[:], ysrc[t0 + j * 128:t0 + (j + 1) * 128, :], writes=[yires])
                            pt, ptres = PTr.next()
                            for k in range(4):
                                S.op("pe", (lambda pt=pt, yi=yi, k=k: lambda e: e.transpose(
                                    pt[:, k, :], yi[:, k * 128:(k + 1) * 128], IDN[:]))(), reads=[yires, rC], writes=[ptres])
                            S.op("act", (lambda yt=yt, pt=pt, j=j: lambda e: e.activation(
                                out=yt[:, :, j * 128:(j + 1) * 128], in_=pt[:, 0:4, :], func=AF.Copy))(),
                                reads=[ptres], writes=[ytres])
                    mt, mtres = MTr.next()
                    for ob in range(8):
                        wb, wbres = WBr.next()
                        S.dma("sp", wb[:], WS_br[l, :, ob].rearrange("j p k c -> p j k c"), writes=[wbres])
                        gt, gtres = GTr.next()
                        S.dma("sp", gt[:, :, 0:ntok], GT[:, t0:t0 + ntok].rearrange("(j o p) t -> o p j t", j=3, p=128)[ob],
                              writes=[gtres])
                        ms = []
                        for b in range(3):
                            yt, ytres = yts[b]
                            ps, pres = PSr.next()
                            for k in range(4):
                                S.op("pe", (lambda ps=ps, wb=wb, yt=yt, b=b, k=k: lambda e: e.matmul(
                                    ps[:, 0:ntok], lhsT=wb[:, b, k, :], rhs=yt[:, k, 0:ntok], start=(k == 0), stop=(k == 3)))(),
                                    reads=[wbres, ytres], writes=[pres])
                            m, mres = (M0r, M1r, M2r)[b].next()
                            S.op("dve", (lambda ps=ps, m=m, gt=gt, b=b: lambda e: e.tensor_tensor(
                                out=m[:, 0:ntok], in0=ps[:, 0:ntok], in1=gt[:, b, 0:ntok], op=ALU.mult))(),
                                reads=[pres, gtres], writes=[mres])
                            ms.append((m, mres))
                        (m0, m0r), (m1, m1r), (m2, m2r) = ms
                        S.op("pool", (lambda m0=m0, m1=m1: lambda e: e.tensor_tensor(
                            out=m0[:, 0:ntok], in0=m0[:, 0:ntok], in1=m1[:, 0:ntok], op=ALU.add))(),
                            reads=[m0r, m1r], writes=[m0r])
                        S.op("pool", (lambda m0=m0, m2=m2, mt=mt, ob=ob: lambda e: e.tensor_tensor(
                            out=mt[:, ob, 0:ntok], in0=m0[:, 0:ntok], in1=m2[:, 0:ntok], op=ALU.add))(),
                            reads=[m0r, m2r], writes=[mtres])
                    HTt, HTres = HTr.next()
                    for j in range(nt):
                        tk = slice(t0 + j * 128, t0 + (j + 1) * 128)
                        xt, xres = XR.next()
                        S.dma("sp", xt[:], xsrc[tk, :], writes=[xres])
                        tt, ttres = TTr.next()
                        s2, s2res = S2r.next()
                        jk, jkres = JK.next()
                        pss = []
                        for nb in range(2):
                            ps, pres = PSr.next()
                            for k in range(8):
                                S.op("pe", (lambda ps=ps, mt=mt, j=j, k=k, nb=nb: lambda e: e.matmul(
                                    ps[:], lhsT=mt[:, k, j * 128:(j + 1) * 128], rhs=WO[:, k, nb * 512:(nb + 1) * 512],
                                    start=(k == 0), stop=(k == 7)))(), reads=[mtres, rWO], writes=[pres])
                            S.op("act", (lambda ps=ps, jk=jk, s2=s2, nb=nb: lambda e: e.activation(
                                out=jk[:, nb * 512:(nb + 1) * 512], in_=ps[:], func=AF.Square, accum_out=s2[:, nb:nb + 1]))(),
                                reads=[pres], writes=[jkres, s2res])
                            pss.append((ps, pres))
                        S.op("dve", (lambda s2=s2: lambda e: e.tensor_tensor(out=s2[:, 2:3], in0=s2[:, 0:1], in1=s2[:, 1:2],
                                                                             op=ALU.add))(), reads=[s2res], writes=[s2res])
                        S.op("act", (lambda s2=s2: lambda e: e.activation(out=s2[:, 3:4], in_=s2[:, 2:3], func=AF.Sqrt,
                                                                          scale=1.0 / D, bias=EPS))(), reads=[s2res], writes=[s2res])
                        S.op("dve", (lambda s2=s2: lambda e: e.reciprocal(out=s2[:, 0:1], in_=s2[:, 3:4]))(),
                             reads=[s2res], writes=[s2res])
                        for nb, (ps, pres) in enumerate(pss):
                            S.op("dve", (lambda ps=ps, s2=s2, tt=tt, nb=nb: lambda e: e.scalar_tensor_tensor(
                                out=tt[:, nb * 512:(nb + 1) * 512], in0=ps[:], scalar=s2[:, 0:1],
                                in1=MODp[:, mset, 0, nb * 512:(nb + 1) * 512], op0=ALU.mult, op1=ALU.mult))(),
                                reads=[pres, s2res, rMOD], writes=[ttres])
                        x1, x1res = X1r.next()
                        S.op("pool", (lambda x1=x1, xt=xt, tt=tt: lambda e: e.tensor_tensor(
                            out=x1[:], in0=xt[:], in1=tt[:], op=ALU.add))(), reads=[xres, ttres], writes=[x1res])
                        S.dma("sp", X1[tk, :], x1[:], reads=[x1res])
                        norm_mod_transpose(x1[:], x1res, MODp[:, mset, 2, :], MODp[:, mset, 1, :], rMOD, HTt, HTres, j, rings)
                    S.dma("sp", H2T[:, :, t0:t0 + ntok], HTt[:, :, 0:ntok], reads=[HTres])
                S.flush()

        def phase_P5b(l):
            need_ctx = l < L - 1
            last = (l == L - 1)
            with ExitStack() as st:
                HTr = Ring(nc, st, "h2", 2, [128, 8, 512], BF16)
                WUr = Ring(nc, st, "wu", 3, [128, 2, 8, 128], BF16)
                PSr = Ring(nc, st, "psu", 3, [128, 512], F32, psum=True)
                RLr = Ring(nc, st, "rl", 3, [128, 512], F32)
                ATr = Ring(nc, st, "at", 2, [128, 32, 512], BF16)
                WDr = Ring(nc, st, "wd", 3, [128, 4, D], BF16)
                PDr = Ring(nc, st, "pd", 4, [128, 512], F32, psum=True)
                XR = Ring(nc, st, "x6", 3, [128, D], F32)
                TTr = Ring(nc, st, "t6", 2, [128, D], F32)
                XOr = Ring(nc, st, "xo", 2, [128, D], F32)
                JK = Ring(nc, st, "jk6", 1, [128, D], BF16)
                S2r = Ring(nc, st, "s26", 4, [128, 4], F32)
                MODp = sbt(st, "MODp6", [128, 2, 1, D], F32)
                rMOD = Res()
                S.dma("sp", MODp[:], MODD[:, :, 5:6, :], writes=[rMOD])
                for (t0, ntok) in supers:
                    if t0 == 0 and not need_ctx:
                        continue
                    mset = 1 if t0 == 0 else 0
                    nt = ntok // 128
                    ht, htres = HTr.next()
                    S.dma("sp", ht[:, :, 0:ntok], H2T[:, :, t0:t0 + ntok], writes=[htres])
                    at, atres = ATr.next()
                    for g0 in range(0, 32, 2):
                        wu, wures = WUr.next()
                        S.dma("sp", wu[:], WS_up[l, g0:g0 + 2].rearrange("b p k c -> p b k c"), writes=[wures])
                        for b in range(2):
                            ps, pres = PSr.next()
                            for k in range(8):
                                S.op("pe", (lambda ps=ps, wu=wu, b=b, k=k: lambda e: e.matmul(
                                    ps[:, 0:ntok], lhsT=wu[:, b, k, :], rhs=ht[:, k, 0:ntok], start=(k == 0), stop=(k == 7)))(),
                                    reads=[wures, htres], writes=[pres])
                            rl, rlres = RLr.next()
                            S.op("act", (lambda ps=ps, rl=rl: lambda e: e.activation(
                                out=rl[:, 0:ntok], in_=ps[:, 0:ntok], func=AF.Relu))(), reads=[pres], writes=[rlres])
                            S.op("pool", (lambda rl=rl, at=at, fb=g0 + b: lambda e: e.tensor_tensor(
                                out=at[:, fb, 0:ntok], in0=rl[:, 0:ntok], in1=rl[:, 0:ntok], op=ALU.mult))(),
                                reads=[rlres], writes=[atres])
                    for jp in range(0, nt, 2):
                        js = list(range(jp, min(jp + 2, nt)))
                        acc = {}
                        for j in js:
                            for nb in range(2):
                                acc[(j, nb)] = PDr.next()
                        for k0 in range(0, 32, 4):
                            wd, wdres = WDr.next()
                            S.dma("sp", wd[:], WM_dn[l, :, k0:k0 + 4, :], writes=[wdres])
                            for j in js:
                                for nb in range(2):
                                    ps, pres = acc[(j, nb)]
                                    for k in range(4):
                                        S.op("pe", (lambda ps=ps, at=at, wd=wd, j=j, k=k, k0=k0, nb=nb: lambda e: e.matmul(
                                            ps[:], lhsT=at[:, k0 + k, j * 128:(j + 1) * 128], rhs=wd[:, k, nb * 512:(nb + 1) * 512],
                                            start=(k0 + k == 0), stop=(k0 + k == 31)))(), reads=[atres, wdres], writes=[pres])
                        for j in js:
                            tk = slice(t0 + j * 128, t0 + (j + 1) * 128)
                            xt, xres = XR.next()
                            S.dma("sp", xt[:], X1[tk, :], writes=[xres])
                            tt, ttres = TTr.next()
                            s2, s2res = S2r.next()
                            jk, jkres = JK.next()
                            for nb in range(2):
                                ps, pres = acc[(j, nb)]
                                S.op("act", (lambda ps=ps, jk=jk, s2=s2, nb=nb: lambda e: e.activation(
                                    out=jk[:, nb * 512:(nb + 1) * 512], in_=ps[:], func=AF.Square,
                                    accum_out=s2[:, nb:nb + 1]))(), reads=[pres], writes=[jkres, s2res])
                            S.op("dve", (lambda s2=s2: lambda e: e.tensor_tensor(out=s2[:, 2:3], in0=s2[:, 0:1], in1=s2[:, 1:2],
                                                                                 op=ALU.add))(), reads=[s2res], writes=[s2res])
                            S.op("act", (lambda s2=s2: lambda e: e.activation(out=s2[:, 3:4], in_=s2[:, 2:3], func=AF.Sqrt,
                                                                              scale=1.0 / D, bias=EPS))(), reads=[s2res], writes=[s2res])
                            S.op("dve", (lambda s2=s2: lambda e: e.reciprocal(out=s2[:, 0:1], in_=s2[:, 3:4]))(),
                                 reads=[s2res], writes=[s2res])
                            for nb in range(2):
                                ps, pres = acc[(j, nb)]
                                S.op("dve", (lambda ps=ps, s2=s2, tt=tt, nb=nb: lambda e: e.scalar_tensor_tensor(
                                    out=tt[:, nb * 512:(nb + 1) * 512], in0=ps[:], scalar=s2[:, 0:1],
                                    in1=MODp[:, mset, 0, nb * 512:(nb + 1) * 512], op0=ALU.mult, op1=ALU.mult))(),
                                    reads=[pres, s2res, rMOD], writes=[ttres])
                            xo, xores = XOr.next()
                            S.op("pool", (lambda xo=xo, xt=xt, tt=tt: lambda e: e.tensor_tensor(
                                out=xo[:], in0=xt[:], in1=tt[:], op=ALU.add))(), reads=[xres, ttres], writes=[xores])
                            if last:
                                S.dma("sp", out[t0 - CTX + j * 128:t0 - CTX + (j + 1) * 128, :], xo[:], reads=[xores])
                            else:
                                S.dma("sp", XS[tk, :], xo[:], reads=[xores])
                S.flush()

        plan = [("W", l) for l in range(L)]
        for l in range(L):
            plan += [("M", l), ("P1", l), ("P2", l), ("P4", l), ("P3", l), ("P5a", l), ("P5b", l)]
        fns = dict(W=phase_W, M=phase_M, P1=phase_P1, P2=phase_P2, P4=phase_P4, P3=phase_P3, P5a=phase_P5a, P5b=phase_P5b)
        for (nm, l) in plan:
            fns[nm](l)
            if stop == "%s_%d" % (nm, l):
                break
        S.op("pool", lambda e: e.memset(LAM[:, 3:4], 0.0), writes=[rC])
        S.flush(final=True)
    return nc


_CACHE = {}


def make_consts(SEQ):
    ident = np.eye(128, dtype=np.float32).astype(ml_dtypes.bfloat16)
    j = np.arange(128)[:, None]
    i = np.arange(128)[None, :]
    masks = np.stack([(i <= j), (j <= i), (j <= i), (j >= i), (j + 32 >= i)]).astype(np.float32).astype(ml_dtypes.bfloat16)
    reset = np.ones((128, 512), np.float32)
    reset[:, ::64] = 0.0
    return dict(rope=rope_tables(SEQ), ident=ident, masks=masks, reset=reset)


def host_inputs(SEQ, b, x, c, ctx, c_ctx, w_ada, b_ada, norm_g, w_in, diff_lambda, diff_subln,
                hgrn_lb, hgrn_norm, swa_sink, w_branch, w_out, w_mlp_up, w_mlp_down, shared):
    f = lambda a: np.ascontiguousarray(np.asarray(a, dtype=np.float32))
    m = dict(shared)
    m["xcat"] = np.ascontiguousarray(np.concatenate([np.asarray(ctx[b]), np.asarray(x[b])], axis=0).astype(np.float32))
    m["c2"] = np.ascontiguousarray(np.stack([np.asarray(c[b]), np.asarray(c_ctx)]).astype(np.float32))
    return m


def shared_inputs(SEQ, w_ada, b_ada, norm_g, w_in, diff_lambda, diff_subln, hgrn_lb, hgrn_norm, swa_sink,
                  w_branch, w_out, w_mlp_up, w_mlp_down):
    f = lambda a: np.ascontiguousarray(np.asarray(a, dtype=np.float32))
    scol, mcol = w_in_column_maps()
    w_in = np.asarray(w_in, dtype=np.float32)
    L = w_in.shape[0]
    m = dict(make_consts(SEQ))
    m.update(w_ada=f(w_ada), b_ada=f(b_ada), norm_g=f(np.asarray(norm_g).reshape(L, 4 * D)),
             w_in_s=np.ascontiguousarray(w_in[:, :, scol]), w_in_m=np.ascontiguousarray(w_in[:, :, mcol]),
             dlam=f(np.asarray(diff_lambda).reshape(L, 256)), dsub=f(diff_subln), hlb=f(hgrn_lb), hnorm=f(hgrn_norm),
             sink=f(swa_sink), w_br=f(w_branch), w_out=f(w_out), w_up=f(w_mlp_up), w_dn=f(w_mlp_down))
    return m


def kernel(x, c, ctx, c_ctx, w_ada, b_ada, norm_g, w_in, diff_lambda, diff_subln,
           hgrn_lb, hgrn_norm, swa_sink, w_branch, w_out, w_mlp_up, w_mlp_down):
    x = np.asarray(x)
    B, SEQ, _ = x.shape
    if SEQ not in _CACHE:
        _CACHE[SEQ] = build_nc(SEQ)
    nc = _CACHE[SEQ]
    shared = shared_inputs(SEQ, w_ada, b_ada, norm_g, w_in, diff_lambda, diff_subln, hgrn_lb, hgrn_norm,
                           swa_sink, w_branch, w_out, w_mlp_up, w_mlp_down)
    ncores = B
    in_maps = []
    for core in range(ncores):
        b = core % B
        in_maps.append(host_inputs(SEQ, b, x, c, ctx, c_ctx, w_ada, b_ada, norm_g, w_in, diff_lambda, diff_subln,
                                   hgrn_lb, hgrn_norm, swa_sink, w_branch, w_out, w_mlp_up, w_mlp_down, shared))
    res = run_bass_kernel_spmd(nc, in_maps, core_ids=list(range(ncores)))
    outs = [np.asarray(res.results[b]["out"], dtype=np.float32) for b in range(B)]
    return np.stack(outs, axis=0)
```

```python
import math
import numpy as np
import ml_dtypes
from contextlib import ExitStack
import concourse.bass as bass
import concourse.mybir as mybir
from concourse.bass_utils import run_bass_kernel_spmd

F32 = mybir.dt.float32
BF16 = mybir.dt.bfloat16
AF = mybir.ActivationFunctionType
ALU = mybir.AluOpType
AX = mybir.AxisListType

ENGS = ("pe", "act", "dve", "pool", "sp")
DMAQ = ("sp", "pool", "act")
KD = 8
ST_Q = "pool"


class Res:
    __slots__ = ("w", "r")

    def __init__(self):
        self.w = None
        self.r = {}


class _Rec:
    def __init__(self):
        self.call = None

    def __getattr__(self, name):
        def f(*a, **k):
            self.call = (name, a, k)
            return self
        return f


def _replay(call):
    name, a, k = call
    return lambda e: getattr(e, name)(*a, **k)


class Sched:
    def __init__(self, nc, stack):
        self.nc = nc
        self.sem = {e: stack.enter_context(nc.semaphore("s_" + e)) for e in ENGS}
        self.dsem = {}
        for q in DMAQ:
            for k in range(KD):
                self.dsem[(q, k)] = stack.enter_context(nc.semaphore("d_%s%d" % (q, k)))
        self.base = {e: 0 for e in ENGS}
        self.dcount = {k: 0 for k in self.dsem}
        self.dstart = dict(self.dcount)
        self.drr = {q: 0 for q in DMAQ}
        self.phase = 0
        self.nops = 0
        self._reset()

    def _reset(self):
        self.ops = {e: [] for e in ENGS}
        self.seen = {e: {} for e in ENGS}

    def _collect(self, eng, reads, writes):
        waits = {}
        seen = self.seen[eng]
        ph = self.phase

        def need(tag, same_ok):
            if tag is None or tag[0] != ph:
                return
            sid, val = tag[1]
            if same_ok and sid == eng:
                return
            if seen.get(sid, 0) >= val:
                return
            if waits.get(sid, 0) < val:
                waits[sid] = val

        for r in reads:
            need(r.w, False)
        for w in writes:
            need(w.w, True)
            for tag in w.r.values():
                need(tag, True)
        for sid, val in waits.items():
            seen[sid] = val
        return waits

    def _commit(self, dep, reads, writes):
        tag = (self.phase, dep)
        for r in reads:
            r.r[dep[0]] = tag
        for w in writes:
            w.w = tag
            w.r = {}

    def op(self, eng, fn, reads=(), writes=()):
        waits = self._collect(eng, reads, writes)
        rec = _Rec()
        fn(rec)
        assert rec.call is not None
        fn = _replay(rec.call)
        self.ops[eng].append([fn, waits, None, False])
        self._commit((eng, len(self.ops[eng])), reads, writes)
        self.nops += 1

    def dma(self, q, out, in_, reads=(), writes=(), **kw):
        if q == "sp" and len(writes) == 0:
            q = ST_Q
        k = self.drr[q] % KD
        self.drr[q] += 1
        slot = (q, k)
        sid = ("d", q, k)
        waits = self._collect(q, reads, writes)
        prev = 16 * self.dcount[slot]
        if prev > 16 * self.dstart[slot] and self.seen[q].get(sid, 0) < prev:
            waits[sid] = max(waits.get(sid, 0), prev)
            self.seen[q][sid] = prev
        self.dcount[slot] += 1
        val = 16 * self.dcount[slot]
        self.ops[q].append([lambda e: e.dma_start(out=out, in_=in_, **kw), waits, slot, False])
        self._commit((sid, val), reads, writes)
        self.nops += 1

    def flush(self, final=False):
        nc = self.nc
        tgt = {e: set() for e in ENGS}
        for e in ENGS:
            for o in self.ops[e]:
                for sid, val in o[1].items():
                    if isinstance(sid, str):
                        tgt[sid].add(val)
        for e in ENGS:
            for i in range(len(self.ops[e]), 0, -1):
                if self.ops[e][i - 1][2] is None:
                    tgt[e].add(i)
                    break
        semval = {}
        newbase = {}
        for e in ENGS:
            v = self.base[e]
            m = {}
            for i, o in enumerate(self.ops[e], start=1):
                if i in tgt[e]:
                    assert o[2] is None
                    v += 1
                    m[i] = v
                    o[3] = True
            semval[e] = m
            newbase[e] = v
        base = dict(self.base)
        dstart = dict(self.dstart)
        dend = dict(self.dcount)
        ops = self.ops
        sem = self.sem
        dsem = self.dsem

        def body(e, ename):
            for f in ENGS:
                if base[f] > 0 and f != ename:
                    e.wait_ge(sem[f], base[f])
            for slot, cnt in dstart.items():
                if cnt > 0:
                    e.wait_ge(dsem[slot], 16 * cnt)
            for o in ops[ename]:
                for sid, val in o[1].items():
                    if isinstance(sid, str):
                        e.wait_ge(sem[sid], semval[sid][val])
                    else:
                        e.wait_ge(dsem[(sid[1], sid[2])], val)
                ins = o[0](e)
                if o[2] is not None:
                    ins.then_inc(dsem[o[2]], 16)
                elif o[3]:
                    ins.then_inc(sem[ename], 1)
            if final:
                for f in ENGS:
                    if newbase[f] > 0 and f != ename:
                        e.wait_ge(sem[f], newbase[f])
                for slot, cnt in dend.items():
                    if cnt > 0:
                        e.wait_ge(dsem[slot], 16 * cnt)

        with nc.Block() as block:
            @block.tensor
            def _(e):
                body(e, "pe")

            @block.scalar
            def _(e):
                body(e, "act")

            @block.vector
            def _(e):
                body(e, "dve")

            @block.gpsimd
            def _(e):
                body(e, "pool")

            @block.sync
            def _(e):
                body(e, "sp")

        self.base = newbase
        self.dstart = dend
        self.phase += 1
        self._reset()


_UID = [0]


class Ring:
    def __init__(self, nc, st, name, n, shape, dtype, psum=False):
        _UID[0] += 1
        if not psum:
            self.t = [st.enter_context(nc.sbuf_tensor("%s_%d_%d" % (name, _UID[0], i), shape, dtype)) for i in range(n)]
        else:
            isz = 2 if dtype == BF16 else 4
            full = [st.enter_context(nc.psum_tensor("%s_%d_%d" % (name, _UID[0], i), [128, 2048 // isz], dtype))
                    for i in range(n)]
            self.t = []
            for f in full:
                fs = 1
                for d_ in shape[1:]:
                    fs *= d_
                v = f[0:shape[0], 0:fs]
                if len(shape) == 3:
                    v = v.rearrange("p (a b) -> p a b", a=shape[1])
                self.t.append(v)
        self.r = [Res() for _ in range(n)]
        self.i = 0
        self.n = n

    def next(self):
        k = self.i % self.n
        self.i += 1
        return self.t[k], self.r[k]


D = 1024
CTX = 256
DEPTH = 2
EPS = 1e-6
NSB = 62
NMC = 1664
SB_KINDS = (["dq"] * 8 + ["dk"] * 8 + ["sq"] * 8 + ["sk"] * 2 +
            ["hq"] * 4 + ["zf"] * 4 + ["zb"] * 4 + ["gate"] * 24)


def w_in_column_maps():
    def rot(cols):
        cols = np.asarray(cols)
        return cols ^ 16
    s = []
    for base, nblk in ((0, 4), (512, 4), (4096, 4), (4608, 1)):
        for b in range(nblk):
            c = np.arange(base + b * 128, base + (b + 1) * 128)
            s.append(c)
            s.append(base + ((c - base) ^ 16))
    for base, nblk in ((1536, 4), (2048, 4), (2560, 4), (4864, 24)):
        for b in range(nblk):
            s.append(np.arange(base + b * 128, base + (b + 1) * 128))
    s = np.concatenate(s)
    m = np.concatenate([np.arange(1024, 1536), np.arange(3072, 3584), np.arange(3584, 4096),
                        np.arange(4736, 4864)])
    assert s.size == NSB * 128 and m.size == NMC
    return s, m


def rope_tables(SEQ):
    T = CTX + SEQ
    t = np.arange(SEQ)
    pos = np.stack([t // 64, t % 64], axis=-1).astype(np.float32)
    inv = (10000.0 ** (-np.arange(16, dtype=np.float32) / 16)).astype(np.float32)
    ang = pos[:, :, None] * inv
    cos = np.cos(ang).astype(np.float32)
    sin = np.sin(ang).astype(np.float32)
    d = np.arange(128) % 64
    half = d // 32
    part = (d % 32) // 16
    f = d % 16
    C = np.ones((128, T), np.float32)
    S_ = np.zeros((128, T), np.float32)
    C[:, CTX:] = cos[:, half, f].T
    sign = np.where(part == 0, -1.0, 1.0).astype(np.float32)
    S_[:, CTX:] = (sin[:, half, f].T) * sign[:, None]
    return np.stack([C, S_, C * 0.125, S_ * 0.125]).astype(np.float32)


def build_nc(SEQ, dbg=False, stop=None):
    T = CTX + SEQ
    NT = T // 128
    NCH = T // 64
    L = DEPTH
    nc = bass.Bass("TRN2", target_bir_lowering=False)

    def din(name, shape, dt=F32):
        return nc.dram_tensor(name, list(shape), dt, kind="ExternalInput").ap()

    def dscr(name, shape, dt):
        if dbg:
            return nc.dram_tensor(name, list(shape), dt, kind="ExternalOutput").ap()
        return nc.dram_tensor(name, list(shape), dt).ap()

    xcat = din("xcat", [T, D])
    c_in = din("c2", [2, D])
    w_ada = din("w_ada", [L, D, 6 * D])
    b_ada = din("b_ada", [L, 6 * D])
    norm_g = din("norm_g", [L, 4 * D])
    w_in_s = din("w_in_s", [L, D, NSB * 128])
    w_in_m = din("w_in_m", [L, D, NMC])
    dlam = din("dlam", [L, 256])
    dsub = din("dsub", [L, 128])
    hlb = din("hlb", [L, 2, 512])
    hnorm = din("hnorm", [L, 128])
    sink = din("sink", [L, 8])
    w_br = din("w_br", [L, 3, 512, D])
    w_out = din("w_out", [L, D, D])
    w_up = din("w_up", [L, D, 4 * D])
    w_dn = din("w_dn", [L, 4 * D, D])
    rope = din("rope", [4, 128, T])
    ident_d = din("ident", [128, 128], BF16)
    masks_d = din("masks", [5, 128, 128], BF16)
    reset_d = din("reset", [128, 512])
    out = nc.dram_tensor("out", [SEQ, D], F32, kind="ExternalOutput").ap()

    XS = dscr("XS", [T, D], F32)
    X1 = dscr("X1", [T, D], F32)
    H2T = dscr("H2T", [128, 8, T], BF16)
    WS_in = dscr("WS_in", [L, NSB, 128, 8, 128], BF16)
    WM_in = dscr("WM_in", [L, 128, 8, NMC], BF16)
    WS_br = dscr("WS_br", [L, 3, 8, 128, 4, 128], BF16)
    WM_out = dscr("WM_out", [L, 128, 8, D], BF16)
    WS_up = dscr("WS_up", [L, 32, 128, 8, 128], BF16)
    WM_dn = dscr("WM_dn", [L, 128, 32, D], BF16)
    QT = dscr("QT", [512, T], BF16)
    KT = dscr("KT", [512, T], BF16)
    SQT = dscr("SQT", [512, T], BF16)
    SKT = dscr("SKT", [128, T], BF16)
    HQT = dscr("HQT", [512, T], F32)
    SGT = dscr("SGT", [2, 512, T], F32)
    GT = dscr("GT", [3072, T], BF16)
    DV = dscr("DV", [T, 4, 129], BF16)
    HV = dscr("HV", [T, 512], BF16)
    HG = dscr("HG", [T, 512], F32)
    SV = dscr("SV", [T, 2, 65], BF16)
    YAT = dscr("YAT", [512, T], BF16)
    YB = dscr("YB", [T, 512], BF16)
    YC = dscr("YC", [T, 512], BF16)
    SPD = dscr("SPD", [2, NCH, 128, 128], BF16)
    DBGQ = dscr("DBGQ", [4, 2, 128, T], BF16)
    DBG1 = dscr("DBG1", [4, 128, 8], F32)
    DBG2 = dscr("DBG2", [4, 128, 128], F32)
    DBG3 = dscr("DBG3", [4, 2, 128, 129], F32)
    MODD = dscr("MODD", [128, 2, 6, D], F32)

    supers = [(0, CTX)] + [(CTX + 512 * i, 512) for i in range(SEQ // 512)]

    with ExitStack() as top:
        S = Sched(nc, top)

        def sbt(st, name, shape, dt):
            _UID[0] += 1
            return st.enter_context(nc.sbuf_tensor("%s_%d" % (name, _UID[0]), list(shape), dt))

        IDN = sbt(top, "IDN", [128, 128], BF16)
        MSK = sbt(top, "MSK", [128, 5, 128], BF16)
        LAM = sbt(top, "LAM", [128, 4], F32)
        SUBC = sbt(top, "SUBC", [128, 1], F32)
        HNW = sbt(top, "HNW", [128, 128], F32)
        ESK = sbt(top, "ESK", [128, 8], F32)
        OML = sbt(top, "OML", [128, 8], F32)
        rC = Res()

        S.dma("sp", IDN[:], ident_d, writes=[rC])
        S.dma("sp", MSK[:], masks_d.rearrange("m p c -> p m c"), writes=[rC])
        S.flush()

        def phase_W(l):
            with ExitStack() as st:
                fr = Ring(nc, st, "wf", 2, [128, 4096], F32)
                br = Ring(nc, st, "wb", 2, [128, 4096], BF16)
                jobs = []
                def stat_jobs(wsrc2d, dst5, nblk, kc, grp):
                    src = wsrc2d.rearrange("(k p) n -> p k n", p=128)
                    for b0 in range(0, nblk, grp):
                        nb = min(grp, nblk - b0)
                        n = kc * nb * 128
                        jobs.append((src[:, :, b0 * 128:(b0 + nb) * 128], n,
                                     ("p (k n) -> p k n", dict(k=kc)),
                                     ("p (k b c) -> p k b c", dict(k=kc, b=nb)),
                                     ("p (b k c) -> p k b c", dict(k=kc, b=nb)),
                                     ("p (b x) -> p b x", dict(b=nb)),
                                     dst5[b0:b0 + nb].rearrange("b p k c -> p b (k c)")))

                def mov_jobs(wsrc2d, dst3, kc_tot, kstep, ncols, cstep):
                    src = wsrc2d.rearrange("(k p) n -> p k n", p=128)
                    for k0 in range(0, kc_tot, kstep):
                        for c0 in range(0, ncols, cstep):
                            nc_ = min(cstep, ncols - c0)
                            n = kstep * nc_
                            v = ("p (k n) -> p k n", dict(k=kstep))
                            jobs.append((src[:, k0:k0 + kstep, c0:c0 + nc_], n, v, v, v, v,
                                         dst3[:, k0:k0 + kstep, c0:c0 + nc_]))

                stat_jobs(w_in_s[l], WS_in[l], NSB, 8, 4)
                mov_jobs(w_in_m[l], WM_in[l], 8, 8, NMC, 512)
                for j in range(3):
                    stat_jobs(w_br[l, j], WS_br[l, j], 8, 4, 8)
                mov_jobs(w_out[l], WM_out[l], 8, 8, D, 512)
                stat_jobs(w_up[l], WS_up[l], 32, 8, 4)
                mov_jobs(w_dn[l], WM_dn[l], 32, 4, D, 1024)
                for i, (sap, n, (pl, kl), (pci, kci), (pco, kco), (ps_, ks_), dap) in enumerate(jobs):
                    ft, fres = fr.next()
                    bt, bres = br.next()
                    S.dma("sp", ft[:, 0:n].rearrange(pl, **kl), sap, writes=[fres])
                    eng = "pool" if i % 2 == 0 else "dve"
                    S.op(eng, lambda e: e.tensor_copy(out=bt[:, 0:n].rearrange(pco, **kco),
                                                      in_=ft[:, 0:n].rearrange(pci, **kci)),
                         reads=[fres], writes=[bres])
                    S.dma("sp", dap, bt[:, 0:n].rearrange(ps_, **ks_), reads=[bres])
                S.flush()

        def phase_M(l):
            lam_init = 0.8 - 0.6 * math.exp(-0.3 * l)
            with ExitStack() as st:
                CT = sbt(st, "CT", [128, 16], F32)
                CA = sbt(st, "CA", [128, 16], F32)
                CAB = sbt(st, "CAB", [128, 16, 128], F32)
                BA = sbt(st, "BA", [128, 6 * D], F32)
                NG = sbt(st, "NG", [128, 4 * D], F32)
                RAW = sbt(st, "RAW", [128, 2, 6 * D], F32)
                war = Ring(nc, st, "wa", 2, [128, 8, 512], F32)
                pr = Ring(nc, st, "pm", 4, [128, 512], F32, psum=True)
                DL = sbt(st, "DL", [128, 256], F32)
                TL = sbt(st, "TL", [128, 128], F32)
                SC = sbt(st, "SC", [128, 8], F32)
                LB = sbt(st, "LB", [128, L, 8], F32)
                LE = sbt(st, "LE", [128, L, 8], F32)
                LS = sbt(st, "LS", [128, 8], F32)
                LR = sbt(st, "LR", [128, 8], F32)
                rT = Res(); rA = Res(); rB = Res(); rBA = Res(); rNG = Res(); rRAW = Res()
                rDL = Res(); rTL = Res(); rSC = Res(); rLB = Res(); rLE = Res(); rLS = Res()
                S.dma("sp", CT[:].rearrange("p (s k) -> p s k", s=2),
                      c_in.rearrange("s (k p) -> p s k", p=128), writes=[rT], allow_slow_non_contiguous=True)
                S.dma("sp", BA[:], b_ada[l:l + 1, :].partition_broadcast(128), writes=[rBA])
                S.dma("sp", NG[:], norm_g[l:l + 1, :].partition_broadcast(128), writes=[rNG])
                S.op("act", lambda e: e.activation(out=CA[:], in_=CT[:], func=AF.Silu), reads=[rT], writes=[rA])
                S.op("dve", lambda e: e.tensor_copy(out=CAB[:], in_=CA[:].unsqueeze(2).broadcast_to([128, 16, 128])),
                     reads=[rA], writes=[rB])
                wsrc = w_ada[l].rearrange("(k p) n -> p k n", p=128)
                for cb in range(12):
                    wt, wres = war.next()
                    S.dma("sp", wt[:], wsrc[:, :, cb * 512:(cb + 1) * 512], writes=[wres])
                    for s in range(2):
                        ps, pres = pr.next()
                        for k in range(8):
                            S.op("pe", (lambda ps=ps, wt=wt, s=s, k=k: lambda e: e.matmul(
                                ps[:], lhsT=CAB[:, s * 8 + k, :], rhs=wt[:, k, :], start=(k == 0), stop=(k == 7)))(),
                                reads=[rB, wres], writes=[pres])
                        S.op("dve", (lambda ps=ps, s=s, cb=cb: lambda e: e.tensor_tensor(
                            out=RAW[:, s, cb * 512:(cb + 1) * 512], in0=ps[:], in1=BA[:, cb * 512:(cb + 1) * 512],
                            op=ALU.add))(), reads=[pres, rBA], writes=[rRAW])
                for s in range(2):
                    m = lambda i, s=s: RAW[:, s, i * D:(i + 1) * D]
                    g = lambda i: NG[:, i * D:(i + 1) * D]
                    S.op("dve", lambda e: e.scalar_tensor_tensor(
                        out=m(1), in0=m(1), scalar=1.0, in1=g(0), op0=ALU.add, op1=ALU.mult),
                        reads=[rRAW, rNG], writes=[rRAW])
                    S.op("dve", lambda e: e.tensor_tensor(out=m(2), in0=m(2), in1=g(1), op=ALU.mult),
                         reads=[rRAW, rNG], writes=[rRAW])
                    S.op("dve", lambda e: e.scalar_tensor_tensor(
                        out=m(4), in0=m(4), scalar=1.0, in1=g(2), op0=ALU.add, op1=ALU.mult),
                        reads=[rRAW, rNG], writes=[rRAW])
                    S.op("dve", lambda e: e.tensor_tensor(out=m(5), in0=m(5), in1=g(3), op=ALU.mult),
                         reads=[rRAW, rNG], writes=[rRAW])
                S.dma("sp", MODD.rearrange("p s i d -> p s (i d)"), RAW[:], reads=[rRAW])
                S.dma("sp", DL[:], dlam[l:l + 1, :].partition_broadcast(128), writes=[rDL])
                S.op("dve", lambda e: e.tensor_tensor(out=TL[:, 0:64], in0=DL[:, 0:64], in1=DL[:, 64:128], op=ALU.mult),
                     reads=[rDL], writes=[rTL])
                S.op("dve", lambda e: e.tensor_tensor(out=TL[:, 64:128], in0=DL[:, 128:192], in1=DL[:, 192:256],
                                                      op=ALU.mult), reads=[rDL], writes=[rTL])
                S.op("dve", lambda e: e.tensor_reduce(out=SC[:, 0:2], in_=TL[:].rearrange("p (a b) -> p a b", a=2),
                                                      axis=AX.X, op=ALU.add), reads=[rTL], writes=[rSC])
                S.op("act", lambda e: e.activation(out=SC[:, 2:4], in_=SC[:, 0:2], func=AF.Exp), reads=[rSC], writes=[rSC])
                S.op("dve", lambda e: e.scalar_tensor_tensor(out=LAM[:, 0:1], in0=SC[:, 3:4], scalar=-lam_init,
                                                             in1=SC[:, 2:3], op0=ALU.add, op1=ALU.subtract),
                     reads=[rSC], writes=[rC])
                S.dma("sp", SUBC[:], dsub[l:l + 1, :].rearrange("o p -> p o"), writes=[rC], allow_slow_non_contiguous=True)
                S.op("dve", lambda e: e.tensor_scalar(out=SUBC[:], in0=SUBC[:], scalar1=1.0 - lam_init, scalar2=None,
                                                      op0=ALU.mult), reads=[rC], writes=[rC])
                S.dma("sp", HNW[:], hnorm[l:l + 1, :].partition_broadcast(128), writes=[rC])
                S.dma("sp", ESK[:], sink[l:l + 1, :].partition_broadcast(128), writes=[rC])
                S.op("act", lambda e: e.activation(out=ESK[:], in_=ESK[:], func=AF.Exp), reads=[rC], writes=[rC])
                S.dma("sp", LB[:].rearrange("p l (r h) -> p l r h", r=2),
                      hlb.rearrange("l r (h p) -> p l r h", p=128), writes=[rLB], allow_slow_non_contiguous=True)
                S.op("act", lambda e: e.activation(out=LE[:], in_=LB[:], func=AF.Exp), reads=[rLB], writes=[rLE])
                S.op("dve", lambda e: e.tensor_copy(out=LS[:], in_=LE[:, 0, :]), reads=[rLE], writes=[rLS])
                for i in range(1, L):
                    S.op("dve", (lambda i=i: lambda e: e.tensor_tensor(out=LS[:], in0=LS[:], in1=LE[:, i, :], op=ALU.add))(),
                         reads=[rLS, rLE], writes=[rLS])
                S.op("dve", lambda e: e.reciprocal(out=LR[:], in_=LS[:]), reads=[rLS], writes=[rLS])
                S.op("pool", lambda e: e.memset(LS[:], 0.0), reads=[rLS], writes=[rLS])
                for i in range(1, l + 1):
                    S.op("dve", (lambda i=i: lambda e: e.tensor_tensor(out=LS[:], in0=LS[:], in1=LE[:, i, :], op=ALU.add))(),
                         reads=[rLS, rLE], writes=[rLS])
                S.op("dve", lambda e: e.tensor_tensor(out=LS[:], in0=LS[:], in1=LR[:], op=ALU.mult), reads=[rLS], writes=[rLS])
                S.op("dve", lambda e: e.tensor_scalar(out=OML[:], in0=LS[:], scalar1=-1.0, scalar2=1.0, op0=ALU.mult,
                                                      op1=ALU.add), reads=[rLS], writes=[rC])
                S.flush()

        def norm_mod_transpose(xt, xres, Gap, SHap, mres, HTt, HTres, j, rings):
            JK, SSr, H1r, Hr, PTr = rings
            jk, jres = JK.next()
            ss, sres = SSr.next()
            h1, h1res = H1r.next()
            hb, hres = Hr.next()
            pt, ptres = PTr.next()
            S.op("act", lambda e: e.activation(out=jk[:], in_=xt, func=AF.Square, accum_out=ss[:, 0:1]),
                 reads=[xres], writes=[jres, sres])
            S.op("act", lambda e: e.activation(out=ss[:, 1:2], in_=ss[:, 0:1], func=AF.Sqrt, scale=1.0 / D, bias=EPS),
                 reads=[sres], writes=[sres])
            S.op("dve", lambda e: e.reciprocal(out=ss[:, 2:3], in_=ss[:, 1:2]), reads=[sres], writes=[sres])
            S.op("dve", lambda e: e.scalar_tensor_tensor(out=h1[:], in0=xt, scalar=ss[:, 2:3], in1=Gap,
                                                         op0=ALU.mult, op1=ALU.mult), reads=[xres, sres, mres], writes=[h1res])
            S.op("pool", lambda e: e.tensor_tensor(out=hb[:], in0=h1[:], in1=SHap, op=ALU.add),
                 reads=[h1res, mres], writes=[hres])
            for k in range(8):
                S.op("pe", (lambda k=k: lambda e: e.transpose(pt[:, k, :], hb[:, k * 128:(k + 1) * 128], IDN[:]))(),
                     reads=[hres, rC], writes=[ptres])
            S.op("act", lambda e: e.activation(out=HTt[:, :, j * 128:(j + 1) * 128], in_=pt[:], func=AF.Copy),
                 reads=[ptres], writes=[HTres])

        def phase_P1(l):
            xsrc = xcat if l == 0 else XS
            with ExitStack() as st:
                XR = Ring(nc, st, "x", 3, [128, D], F32)
                JK = Ring(nc, st, "jk", 1, [128, D], BF16)
                SSr = Ring(nc, st, "ss", 4, [128, 4], F32)
                H1r = Ring(nc, st, "h1", 2, [128, D], F32)
                Hr = Ring(nc, st, "hb", 2, [128, D], BF16)
                PTr = Ring(nc, st, "ptr", 2, [128, 8, 128], BF16, psum=True)
                HTr = Ring(nc, st, "ht", 2, [128, 8, 512], BF16)
                WSr = Ring(nc, st, "ws", 4, [128, 2, 8, 128], BF16)
                WMr = Ring(nc, st, "wm", 2, [128, 8, 512], BF16)
                RPr = Ring(nc, st, "rp", 2, [128, 4, 512], F32)
                PSr = Ring(nc, st, "ps", 4, [128, 512], F32, psum=True)
                T1r = Ring(nc, st, "t1", 2, [128, 512], F32)
                T2r = Ring(nc, st, "t2", 2, [128, 512], F32)
                OBr = Ring(nc, st, "ob", 3, [128, 512], BF16)
                OFr = Ring(nc, st, "of", 3, [128, 512], F32)
                VAr = Ring(nc, st, "va", 2, [128, 4, 129], BF16)
                SVr = Ring(nc, st, "sv", 2, [128, 2, 65], BF16)
                for t_, r_ in zip(VAr.t, VAr.r):
                    S.op("pool", (lambda t_=t_: lambda e: e.memset(t_[:, :, 128:129], 1.0))(), writes=[r_])
                for t_, r_ in zip(SVr.t, SVr.r):
                    S.op("pool", (lambda t_=t_: lambda e: e.memset(t_[:, :, 64:65], 1.0))(), writes=[r_])
                rings = (JK, SSr, H1r, Hr, PTr)
                MODp = sbt(st, "MODp", [128, 2, 2, D], F32)
                rMOD = Res()
                S.dma("sp", MODp[:], MODD[:, :, 0:2, :], writes=[rMOD])
                for (t0, ntok) in supers:
                    mset = 1 if t0 == 0 else 0
                    nt = ntok // 128
                    HTt, HTres = HTr.next()
                    for j in range(nt):
                        xt, xres = XR.next()
                        S.dma("sp", xt[:], xsrc[t0 + j * 128:t0 + (j + 1) * 128, :], writes=[xres])
                        norm_mod_transpose(xt[:], xres, MODp[:, mset, 1, :], MODp[:, mset, 0, :], rMOD, HTt, HTres, j, rings)
                    rp, rpres = RPr.next()
                    S.dma("sp", rp[:, :, 0:ntok], rope[:, :, t0:t0 + ntok].rearrange("a p t -> p a t"), writes=[rpres])
                    for g0 in range(0, NSB, 2):
                        wt, wres = WSr.next()
                        S.dma("sp", wt[:], WS_in[l, g0:g0 + 2].rearrange("b p k c -> p b k c"), writes=[wres])
                        kind = SB_KINDS[g0]
                        pss = []
                        for b in range(2):
                            ps, pres = PSr.next()
                            for k in range(8):
                                S.op("pe", (lambda ps=ps, wt=wt, b=b, k=k: lambda e: e.matmul(
                                    ps[:, 0:ntok], lhsT=wt[:, b, k, :], rhs=HTt[:, k, 0:ntok],
                                    start=(k == 0), stop=(k == 7)))(), reads=[wres, HTres], writes=[pres])
                            pss.append((ps, pres))
                        if kind in ("dq", "dk", "sq", "sk"):
                            (pa, pares), (pb, pbres) = pss
                            ci = 2 if kind in ("dq", "sq") else 0
                            t1, t1res = T1r.next()
                            t2, t2res = T2r.next()
                            ob, obres = OBr.next()
                            S.op("dve", (lambda pa=pa, t1=t1, ci=ci: lambda e: e.tensor_tensor(
                                out=t1[:, 0:ntok], in0=pa[:, 0:ntok], in1=rp[:, ci, 0:ntok], op=ALU.mult))(),
                                reads=[pares, rpres], writes=[t1res])
                            S.op("dve", (lambda pb=pb, t2=t2, ci=ci: lambda e: e.tensor_tensor(
                                out=t2[:, 0:ntok], in0=pb[:, 0:ntok], in1=rp[:, ci + 1, 0:ntok], op=ALU.mult))(),
                                reads=[pbres, rpres], writes=[t2res])
                            S.op("pool", (lambda t1=t1, t2=t2, ob=ob: lambda e: e.tensor_tensor(
                                out=ob[:, 0:ntok], in0=t1[:, 0:ntok], in1=t2[:, 0:ntok], op=ALU.add))(),
                                reads=[t1res, t2res], writes=[obres])
                            pair = g0 // 2
                            if kind == "dq":
                                dst = QT[pair * 128:(pair + 1) * 128, t0:t0 + ntok]
                            elif kind == "dk":
                                dst = KT[(pair - 4) * 128:(pair - 3) * 128, t0:t0 + ntok]
                            elif kind == "sq":
                                dst = SQT[(pair - 8) * 128:(pair - 7) * 128, t0:t0 + ntok]
                            else:
                                dst = SKT[:, t0:t0 + ntok]
                            S.dma("sp", dst, ob[:, 0:ntok], reads=[obres])
                        else:
                            for b, (ps, pres) in enumerate(pss):
                                blk = g0 + b
                                if kind == "gate":
                                    ob, obres = OBr.next()
                                    S.op("act", (lambda ps=ps, ob=ob: lambda e: e.activation(
                                        out=ob[:, 0:ntok], in_=ps[:, 0:ntok], func=AF.Sigmoid))(),
                                        reads=[pres], writes=[obres])
                                    r0 = (blk - 38) * 128
                                    S.dma("sp", GT[r0:r0 + 128, t0:t0 + ntok], ob[:, 0:ntok], reads=[obres])
                                else:
                                    of, ofres = OFr.next()
                                    if kind == "hq":
                                        S.op("act", (lambda ps=ps, of=of: lambda e: e.activation(
                                            out=of[:, 0:ntok], in_=ps[:, 0:ntok], func=AF.Silu))(),
                                            reads=[pres], writes=[ofres])
                                        r0 = (blk - 26) * 128
                                        dst = HQT[r0:r0 + 128, t0:t0 + ntok]
                                    else:
                                        S.op("act", (lambda ps=ps, of=of: lambda e: e.activation(
                                            out=of[:, 0:ntok], in_=ps[:, 0:ntok], func=AF.Sigmoid, scale=-1.0))(),
                                            reads=[pres], writes=[ofres])
                                        if kind == "zf":
                                            r0 = (blk - 30) * 128
                                            dst = SGT[0, r0:r0 + 128, t0:t0 + ntok]
                                        else:
                                            r0 = (blk - 34) * 128
                                            dst = SGT[1, r0:r0 + 128, t0:t0 + ntok]
                                    S.dma("sp", dst, of[:, 0:ntok], reads=[ofres])
                    for cb, (c0, ncol) in enumerate(((0, 512), (512, 512), (1024, 512), (1536, 128))):
                        wm, wmres = WMr.next()
                        S.dma("sp", wm[:, :, 0:ncol], WM_in[l, :, :, c0:c0 + ncol], writes=[wmres])
                        for j in range(nt):
                            ps, pres = PSr.next()
                            for k in range(8):
                                S.op("pe", (lambda ps=ps, wm=wm, j=j, k=k, ncol=ncol: lambda e: e.matmul(
                                    ps[:, 0:ncol], lhsT=HTt[:, k, j * 128:(j + 1) * 128], rhs=wm[:, k, 0:ncol],
                                    start=(k == 0), stop=(k == 7)))(), reads=[wmres, HTres], writes=[pres])
                            tk = slice(t0 + j * 128, t0 + (j + 1) * 128)
                            if cb == 0:
                                va, vares = VAr.next()
                                S.op("act", (lambda ps=ps, va=va: lambda e: e.activation(
                                    out=va[:, :, 0:128], in_=ps[:].rearrange("p (h c) -> p h c", h=4), func=AF.Copy))(),
                                    reads=[pres], writes=[vares])
                                S.dma("sp", DV[tk], va[:], reads=[vares])
                            elif cb == 1:
                                ob, obres = OBr.next()
                                S.op("act", (lambda ps=ps, ob=ob: lambda e: e.activation(
                                    out=ob[:], in_=ps[:], func=AF.Copy))(), reads=[pres], writes=[obres])
                                S.dma("sp", HV[tk, :], ob[:], reads=[obres])
                            elif cb == 2:
                                of, ofres = OFr.next()
                                S.op("act", (lambda ps=ps, of=of: lambda e: e.activation(
                                    out=of[:], in_=ps[:], func=AF.Silu))(), reads=[pres], writes=[ofres])
                                S.dma("sp", HG[tk, :], of[:], reads=[ofres])
                            else:
                                sv, svres = SVr.next()
                                S.op("act", (lambda ps=ps, sv=sv: lambda e: e.activation(
                                    out=sv[:, :, 0:64], in_=ps[:, 0:128].rearrange("p (h c) -> p h c", h=2),
                                    func=AF.Copy))(), reads=[pres], writes=[svres])
                                S.dma("sp", SV[tk], sv[:], reads=[svres])
                S.flush()

        def phase_P2(l):
            need_ctx = l < L - 1
            with ExitStack() as st:
                KTh = sbt(st, "KTh", [128, T], BF16)
                Vh = sbt(st, "Vh", [128, NT, 128], BF16)
                ONES = sbt(st, "ONES", [128, 128], F32)
                rK = Res(); rV = Res(); rO = Res()
                S.op("pool", lambda e: e.memset(ONES[:], 1.0), writes=[rO])
                ONESB = sbt(st, "ONESB", [128, 128], BF16)
                S.op("pool", lambda e: e.memset(ONESB[:], 1.0), writes=[rO])
                RQr = Ring(nc, st, "rq", 1, [128, 512], F32, psum=True)
                QBr = Ring(nc, st, "qb", 2, [128, 512], BF16)
                STr = [Ring(nc, st, "st%d" % c, 2, [128, 512], F32, psum=True) for c in range(2)]
                PTr = [Ring(nc, st, "pt%d" % c, 4, [128, 512], BF16) for c in range(2)]
                OTr = [Ring(nc, st, "ot%d" % c, 1, [128, 512], F32, psum=True) for c in range(2)]
                RSr = Ring(nc, st, "rsp", 1, [128, 512], F32, psum=True)
                RAr = [Ring(nc, st, "ra%d" % c, 2, [128, 512], F32) for c in range(2)]
                RRr = [Ring(nc, st, "rr%d" % c, 1, [128, 512], F32) for c in range(2)]
                O0r = Ring(nc, st, "o0", 1, [128, 512], F32)
                T1r = Ring(nc, st, "t1p", 1, [128, 512], F32)
                OOr = Ring(nc, st, "oo", 2, [128, 512], F32)
                SQr = Ring(nc, st, "sqq", 1, [128, 512], F32)
                SDr = Ring(nc, st, "sd", 1, [128, 512], F32)
                YAr = Ring(nc, st, "ya", 2, [128, 512], BF16)
                qblocks = ([(0, CTX, [0, 1])] if need_ctx else []) + \
                          [(CTX + 512 * i, 512, list(range(NT))) for i in range(SEQ // 512)]
                acc_eng = ("pool", "dve")
                for h in range(4):
                    S.dma("sp", KTh[:], KT[h * 128:(h + 1) * 128, :], writes=[rK])
                    S.dma("sp", Vh[:], DV[:, h, 0:128].rearrange("(k p) c -> p k c", p=128), writes=[rV])
                    for (q0, nq, kts) in qblocks:
                        qb, qres = QBr.next()
                        S.dma("sp", qb[:, 0:nq], QT[h * 128:(h + 1) * 128, q0:q0 + nq], writes=[qres])
                        ots = [OTr[c].next() for c in range(2)]
                        ras = [RAr[c].next() for c in range(2)]
                        rq, rqres = RQr.next()
                        def qk_exp(kt):
                            pts = []
                            for c in range(2):
                                stt, stres = STr[c].next()
                                S.op("pe", lambda e: e.matmul(
                                    stt[:, 0:nq], lhsT=KTh[c * 64:(c + 1) * 64, kt * 128:(kt + 1) * 128],
                                    rhs=qb[c * 64:(c + 1) * 64, 0:nq], start=True, stop=True),
                                    reads=[rK, qres], writes=[stres])
                                pt, ptres = PTr[c].next()
                                S.op("act", lambda e: e.activation(out=pt[:, 0:nq], in_=stt[:, 0:nq], func=AF.Exp),
                                     reads=[stres], writes=[ptres])
                                pts.append((pt, ptres))
                            return pts

                        pend = qk_exp(kts[0])
                        for ki, kt in enumerate(kts):
                            pts = pend
                            pend = qk_exp(kts[ki + 1]) if ki + 1 < len(kts) else None
                            for c in range(2):
                                pt, ptres = pts[c]
                                ot, otres = ots[c]
                                S.op("pe", lambda e: e.matmul(ot[:, 0:nq], lhsT=Vh[:, kt, :], rhs=pt[:, 0:nq],
                                                              start=(ki == 0), stop=(ki == len(kts) - 1)),
                                     reads=[ptres, rV], writes=[otres])
                                if c == 1:
                                    S.op("pe", lambda e: e.matmul(rq[:, 0:nq], lhsT=ONESB[:], rhs=pt[:, 0:nq],
                                                                  start=(ki == 0), stop=(ki == len(kts) - 1)),
                                         reads=[ptres, rO], writes=[rqres])
                                else:
                                    ra, rares = ras[ki % 2]
                                    if ki < 2:
                                        S.op(acc_eng[ki % 2], lambda e: e.tensor_copy(out=ra[:, 0:nq], in_=pt[:, 0:nq]),
                                             reads=[ptres], writes=[rares])
                                    else:
                                        S.op(acc_eng[ki % 2], lambda e: e.tensor_tensor(
                                            out=ra[:, 0:nq], in0=ra[:, 0:nq], in1=pt[:, 0:nq], op=ALU.add),
                                            reads=[ptres, rares], writes=[rares])
                        rrs = []
                        (raa, raares), (rab, rabres) = ras
                        S.op("dve", lambda e: e.tensor_tensor(out=raa[:, 0:nq], in0=raa[:, 0:nq], in1=rab[:, 0:nq], op=ALU.add),
                             reads=[raares, rabres], writes=[raares])
                        rs, rsres = RSr.next()
                        S.op("pe", lambda e: e.matmul(rs[:, 0:nq], lhsT=ONES[:], rhs=raa[:, 0:nq], start=True, stop=True),
                             reads=[rO, raares], writes=[rsres])
                        rr, rrres = RRr[0].next()
                        S.op("dve", lambda e: e.reciprocal(out=rr[:, 0:nq], in_=rs[:, 0:nq]), reads=[rsres], writes=[rrres])
                        rrs.append((rr, rrres))
                        rr, rrres = RRr[1].next()
                        S.op("dve", lambda e: e.reciprocal(out=rr[:, 0:nq], in_=rq[:, 0:nq]), reads=[rqres], writes=[rrres])
                        rrs.append((rr, rrres))
                        o0, o0res = O0r.next()
                        t1, t1res = T1r.next()
                        oo, oores = OOr.next()
                        sq, sqres = SQr.next()
                        sd, sdres = SDr.next()
                        ya, yares = YAr.next()
                        S.op("dve", lambda e: e.tensor_tensor(out=o0[:, 0:nq], in0=ots[0][0][:, 0:nq], in1=rrs[0][0][:, 0:nq],
                                                              op=ALU.mult), reads=[ots[0][1], rrs[0][1]], writes=[o0res])
                        S.op("dve", lambda e: e.tensor_tensor(out=t1[:, 0:nq], in0=ots[1][0][:, 0:nq], in1=rrs[1][0][:, 0:nq],
                                                              op=ALU.mult), reads=[ots[1][1], rrs[1][1]], writes=[t1res])
                        S.op("dve", lambda e: e.scalar_tensor_tensor(out=oo[:, 0:nq], in0=t1[:, 0:nq], scalar=LAM[:, 0:1],
                                                                      in1=o0[:, 0:nq], op0=ALU.mult, op1=ALU.add),
                             reads=[t1res, o0res, rC], writes=[oores])
                        S.op("pool", lambda e: e.tensor_tensor(out=sq[:, 0:nq], in0=oo[:, 0:nq], in1=oo[:, 0:nq], op=ALU.mult),
                             reads=[oores], writes=[sqres])
                        rs, rsres = RSr.next()
                        S.op("pe", lambda e: e.matmul(rs[:, 0:nq], lhsT=ONES[:], rhs=sq[:, 0:nq], start=True, stop=True),
                             reads=[rO, sqres], writes=[rsres])
                        S.op("act", lambda e: e.activation(out=sd[:, 0:nq], in_=rs[:, 0:nq], func=AF.Sqrt, scale=1.0 / 128, bias=EPS),
                             reads=[rsres], writes=[sdres])
                        S.op("dve", lambda e: e.reciprocal(out=sd[:, 0:nq], in_=sd[:, 0:nq]), reads=[sdres], writes=[sdres])
                        S.op("dve", lambda e: e.scalar_tensor_tensor(out=ya[:, 0:nq], in0=oo[:, 0:nq], scalar=SUBC[:, 0:1],
                                                                     in1=sd[:, 0:nq], op0=ALU.mult, op1=ALU.mult),
                             reads=[oores, sdres, rC], writes=[yares])
                        S.dma("sp", YAT[h * 128:(h + 1) * 128, q0:q0 + nq], ya[:, 0:nq], reads=[yares])
                S.flush()

        def phase_P4(l):
            need_ctx = l < L - 1
            with ExitStack() as st:
                SKg = sbt(st, "SKg", [64, T], BF16)
                SVg = sbt(st, "SVg", [128, NT, 65], BF16)
                rK = Res(); rV = Res()
                SQr = Ring(nc, st, "sq", 3, [64, 4, 128], BF16)
                STr = Ring(nc, st, "sst", 5, [128, 512], F32, psum=True)
                PTr = Ring(nc, st, "spt", 8, [128, 4, 128], BF16)
                Or = Ring(nc, st, "so", 2, [128, 4, 65], F32, psum=True)
                DNr = Ring(nc, st, "dn", 3, [128, 8], F32)
                YCr = Ring(nc, st, "yc", 3, [128, 4, 64], BF16)
                for g in range(2):
                    S.dma("sp", SKg[:], SKT[g * 64:(g + 1) * 64, :], writes=[rK])
                    S.dma("sp", SVg[:], SV[:, g, :].rearrange("(k p) c -> p k c", p=128), writes=[rV])
                    for qt in range(0 if need_ctx else 2, NT):
                        if qt < 2:
                            keys = [(0, None), (1, None)]
                        else:
                            keys = [(0, None), (1, None)]
                            if qt - 1 >= 2:
                                keys.append((qt - 1, 0))
                            keys.append((qt, None))
                            if qt + 1 < NT:
                                keys.append((qt + 1, 1))
                        sq, sqres = SQr.next()
                        S.dma("sp", sq[:], SQT[g * 256:(g + 1) * 256, qt * 128:(qt + 1) * 128].rearrange(
                            "(i d) t -> d i t", d=64), writes=[sqres])
                        o, ores = Or.next()
                        pts = []
                        for ki, (kt, mk) in enumerate(keys):
                            stt, stres = STr.next()
                            pt, ptres = PTr.next()
                            S.op("pe", lambda e: e.matmul(
                                stt[:], lhsT=SKg[:, kt * 128:(kt + 1) * 128], rhs=sq[:].rearrange("d i t -> d (i t)"),
                                start=True, stop=True), reads=[rK, sqres], writes=[stres])
                            S.op("act", lambda e: e.activation(
                                out=pt[:].rearrange("p i t -> p (i t)"), in_=stt[:], func=AF.Exp),
                                reads=[stres], writes=[ptres])
                            if mk is not None:
                                S.op("pool", lambda e: e.tensor_tensor(
                                    out=pt[:], in0=pt[:], in1=MSK[:, mk:mk + 1, :].broadcast_to([128, 4, 128]),
                                    op=ALU.mult), reads=[ptres, rC], writes=[ptres])
                            pts.append((pt, ptres))
                        for ki, (kt, mk) in enumerate(keys):
                            pt, ptres = pts[ki]
                            for i in range(4):
                                S.op("pe", lambda e: e.matmul(
                                    o[:, i, :], lhsT=pt[:, i, :], rhs=SVg[:, kt, :],
                                    start=(ki == 0 and i == 0), stop=(ki == len(keys) - 1), skip_group_check=True),
                                    reads=[ptres, rV], writes=[ores])
                        dn, dnres = DNr.next()
                        yc, ycres = YCr.next()
                        S.op("dve", (lambda o=o, dn=dn, g=g: lambda e: e.tensor_tensor(
                            out=dn[:, 0:4], in0=o[:, :, 64], in1=ESK[:, g * 4:(g + 1) * 4], op=ALU.add))(),
                            reads=[ores, rC], writes=[dnres])
                        S.op("dve", (lambda dn=dn: lambda e: e.reciprocal(out=dn[:, 4:8], in_=dn[:, 0:4]))(),
                             reads=[dnres], writes=[dnres])
                        S.op("dve", (lambda o=o, dn=dn, yc=yc: lambda e: e.tensor_tensor(
                            out=yc[:], in0=o[:, :, 0:64], in1=dn[:, 4:8].unsqueeze(2).broadcast_to([128, 4, 64]),
                            op=ALU.mult))(), reads=[ores, dnres], writes=[ycres])
                        S.dma("sp", YC[qt * 128:(qt + 1) * 128, g * 256:(g + 1) * 256],
                              yc[:].rearrange("p i d -> p (i d)"), reads=[ycres])
                S.flush()

        def phase_P3(l):
            need_ctx = l < L - 1
            SEGN = 32
            blocks = [(0, CTX // 64)] + [(CTX + 512 * i, 8) for i in range(SEQ // 512)]
            order = {0: [(b, list(range(b[1]))) for b in blocks],
                     1: [(blocks[0], list(range(blocks[0][1]))[::-1])] +
                        [(b, list(range(b[1]))[::-1]) for b in blocks[1:][::-1]]}
            pos_of = {0: {}, 1: {}}
            for r in range(2):
                p = 0
                for (t0, nch), chs in order[r]:
                    for ci in chs:
                        pos_of[r][t0 // 64 + ci] = p
                        p += 1
            with ExitStack() as st:
                RST = sbt(st, "RST", [128, 512], F32)
                rRST = Res()
                S.dma("sp", RST[:], reset_d, writes=[rRST])
                QET = [sbt(st, "QET%d" % r, [128, T], BF16) for r in range(2)]
                rQET = [[Res() for _ in range(len(blocks))] for r in range(2)]
                ATA = [sbt(st, "ATA%d" % r, [32, NCH, 64], BF16) for r in range(2)]
                ATB = [sbt(st, "ATB%d" % r, [32, NCH, 64], BF16) for r in range(2)]
                rATB0 = Res()
                for r in range(2):
                    S.op("pool", (lambda r=r: lambda e: e.memset(ATB[r][:], 0.0))(), writes=[rATB0])
                rAT = [[Res() for _ in range(len(blocks))] for r in range(2)]
                SGr = Ring(nc, st, "sg", 2, [128, 512], F32)
                HQr = Ring(nc, st, "hq", 2, [128, 512], F32)
                KKr = Ring(nc, st, "kk", 1, [128, 512], F32)
                LFr = Ring(nc, st, "lf", 1, [128, 512], F32)
                BBr = Ring(nc, st, "bb", 1, [128, 512], F32)
                CCr = Ring(nc, st, "cc", 1, [128, 512], F32)
                EPr = Ring(nc, st, "ep", 2, [128, 512], F32)
                ENr = Ring(nc, st, "en", 2, [128, 512], F32)
                KEr = Ring(nc, st, "ke", 2, [128, 512], BF16)
                AXr = Ring(nc, st, "ax", 2, [128, 3, 8], F32)
                EXr = Ring(nc, st, "ex", 2, [128, 3, 8], F32)
                VBr = Ring(nc, st, "vb", 3, [64, 8, 128], BF16)
                VHr = Ring(nc, st, "vh", 2, [32, 16, 128], BF16)
                KCr = Ring(nc, st, "kc", 3, [64, 128], BF16)
                PKr = Ring(nc, st, "pk", 2, [64, 128], BF16, psum=True)
                PUr = Ring(nc, st, "pu", 2, [128, 128], F32, psum=True)
                PAr = Ring(nc, st, "pa", 2, [32, 128], F32, psum=True)
                POr = Ring(nc, st, "po", 2, [64, 128], F32, psum=True)
                EUr = Ring(nc, st, "eu", 2, [128, SEGN, 128], F32)
                EUS = [[Res() for _ in range(SEGN)] for _ in range(2)]
                SCr = Ring(nc, st, "scs", 2, [128, 2, SEGN], F32)
                SPr = Ring(nc, st, "sps", 1, [128, SEGN, 128], BF16)
                CARRY = [sbt(st, "CAR%d" % i, [128, 128], F32) for i in range(2)]
                rCAR = [Res(), Res()]
                SFr = Ring(nc, st, "sf", 4, [128, 8, 128], BF16)
                GGr = Ring(nc, st, "gg", 2, [64, 8, 128], F32)
                RSr = Ring(nc, st, "rs", 4, [64, 4], F32)
                JKr = Ring(nc, st, "jk3", 2, [64, 128], BF16)
                Y1r = Ring(nc, st, "y1", 2, [64, 128], F32)
                YBr = Ring(nc, st, "yb", 2, [64, 8, 128], BF16)
                for h in range(4):
                    for r in range(2):
                        car = 0
                        S.op("pool", (lambda car=car: lambda e: e.memset(CARRY[car][:], 0.0))(), writes=[rCAR[car]])
                        seg = []
                        eu, eures = EUr.next()
                        eus = EUS[(EUr.i - 1) % EUr.n]
                        seg_chunks = []
                        sc, scres = SCr.next()
                        nslot = 0
                        total = sum(len(chs) for _, chs in order[r])
                        done = 0
                        for bi_, ((t0, nch), chs) in enumerate(order[r]):
                            bidx = blocks.index((t0, nch))
                            nb = nch * 64
                            sg, sgres = SGr.next()
                            hq, hqres = HQr.next()
                            kk, kkres = KKr.next()
                            lf, lfres = LFr.next()
                            bb, bbres = BBr.next()
                            cc, ccres = CCr.next()
                            ep, epres = EPr.next()
                            en, enres = ENr.next()
                            ke, keres = KEr.next()
                            ax, axres = AXr.next()
                            ex, exres = EXr.next()
                            vb, vbres = VBr.next()
                            S.dma("sp", sg[:, 0:nb], SGT[r, h * 128:(h + 1) * 128, t0:t0 + nb], writes=[sgres])
                            S.dma("sp", hq[:, 0:nb], HQT[h * 128:(h + 1) * 128, t0:t0 + nb], writes=[hqres])
                            S.dma("sp", vb[:, 0:nch, :], HV[t0:t0 + nb, h * 128:(h + 1) * 128].rearrange(
                                "(c s) d -> s c d", s=64), writes=[vbres])
                            oc = r * 4 + h
                            S.op("dve", (lambda kk=kk, sg=sg, nb=nb, oc=oc: lambda e: e.tensor_scalar(
                                out=kk[:, 0:nb], in0=sg[:, 0:nb], scalar1=OML[:, oc:oc + 1], scalar2=None, op0=ALU.mult))(),
                                reads=[sgres, rC], writes=[kkres])
                            S.op("act", (lambda lf=lf, kk=kk, nb=nb: lambda e: e.activation(
                                out=lf[:, 0:nb], in_=kk[:, 0:nb], func=AF.Ln, scale=-1.0, bias=1.0))(),
                                reads=[kkres], writes=[lfres])
                            S.op("dve", (lambda bb=bb, lf=lf, nb=nb: lambda e: e.tensor_tensor_scan(
                                out=bb[:, 0:nb], data0=RST[:, 0:nb], data1=lf[:, 0:nb], initial=0.0,
                                op0=ALU.mult, op1=ALU.add))(), reads=[lfres, rRST], writes=[bbres])
                            v3 = lambda t, nch=nch: t[:, 0:nch * 64].rearrange("p (c t) -> p c t", t=64)
                            if r == 0:
                                S.op("dve", (lambda cc=cc, bb=bb, nch=nch, v3=v3: lambda e: e.tensor_tensor(
                                    out=v3(cc), in0=v3(bb), in1=v3(bb)[:, :, 31:32].broadcast_to([128, nch, 64]),
                                    op=ALU.subtract))(), reads=[bbres], writes=[ccres])
                                S.op("pool", (lambda ax=ax, bb=bb, nch=nch, v3=v3: lambda e: e.tensor_copy(
                                    out=ax[:, 0, 0:nch], in_=v3(bb)[:, :, 63]))(), reads=[bbres], writes=[axres])
                                S.op("pool", (lambda ax=ax, bb=bb, nch=nch, v3=v3: lambda e: e.tensor_copy(
                                    out=ax[:, 1, 0:nch], in_=v3(bb)[:, :, 31]))(), reads=[bbres], writes=[axres])
                                S.op("pool", (lambda ax=ax, nch=nch: lambda e: e.tensor_tensor(
                                    out=ax[:, 2, 0:nch], in0=ax[:, 0, 0:nch], in1=ax[:, 1, 0:nch], op=ALU.subtract))(),
                                    reads=[axres], writes=[axres])
                            else:
                                S.op("dve", (lambda bb=bb, lf=lf, kk=kk, nb=nb: lambda e: e.tensor_tensor(
                                    out=lf[:, 0:nb], in0=bb[:, 0:nb], in1=lf[:, 0:nb], op=ALU.subtract))(),
                                    reads=[bbres, lfres], writes=[lfres])
                                S.op("dve", (lambda cc=cc, lf=lf, nch=nch, v3=v3: lambda e: e.tensor_tensor(
                                    out=v3(cc), in0=v3(lf), in1=v3(lf)[:, :, 31:32].broadcast_to([128, nch, 64]),
                                    op=ALU.subtract))(), reads=[lfres], writes=[ccres])
                                S.op("pool", (lambda ax=ax, bb=bb, nch=nch, v3=v3: lambda e: e.tensor_copy(
                                    out=ax[:, 0, 0:nch], in_=v3(bb)[:, :, 63]))(), reads=[bbres], writes=[axres])
                                S.op("pool", (lambda ax=ax, lf=lf, nch=nch, v3=v3: lambda e: e.tensor_copy(
                                    out=ax[:, 2, 0:nch], in_=v3(lf)[:, :, 31]))(), reads=[lfres], writes=[axres])
                                S.op("pool", (lambda ax=ax, nch=nch: lambda e: e.tensor_tensor(
                                    out=ax[:, 1, 0:nch], in0=ax[:, 0, 0:nch], in1=ax[:, 2, 0:nch], op=ALU.subtract))(),
                                    reads=[axres], writes=[axres])
                            S.op("act", (lambda ex=ex, ax=ax: lambda e: e.activation(out=ex[:], in_=ax[:], func=AF.Exp))(),
                                 reads=[axres], writes=[exres])
                            S.op("act", (lambda ep=ep, cc=cc, nb=nb: lambda e: e.activation(
                                out=ep[:, 0:nb], in_=cc[:, 0:nb], func=AF.Exp))(), reads=[ccres], writes=[epres])
                            S.op("act", (lambda en=en, cc=cc, nb=nb: lambda e: e.activation(
                                out=en[:, 0:nb], in_=cc[:, 0:nb], func=AF.Exp, scale=-1.0))(), reads=[ccres], writes=[enres])
                            qmul, qmr = (ep, epres) if r == 0 else (en, enres)
                            kmul, kmr = (en, enres) if r == 0 else (ep, epres)
                            S.op("pool", (lambda qmul=qmul, hq=hq, nb=nb, t0=t0, r=r: lambda e: e.tensor_tensor(
                                out=QET[r][:, t0:t0 + nb], in0=hq[:, 0:nb], in1=qmul[:, 0:nb], op=ALU.mult))(),
                                reads=[hqres, qmr], writes=[rQET[r][bidx]])
                            S.op("dve", (lambda kmul=kmul, kk=kk, ke=ke, nb=nb: lambda e: e.tensor_tensor(
                                out=ke[:, 0:nb], in0=kk[:, 0:nb], in1=kmul[:, 0:nb], op=ALU.mult))(),
                                reads=[kkres, kmr], writes=[keres])
                            for ci in chs:
                                cg = t0 // 64 + ci
                                cs = slice(ci * 64, (ci + 1) * 64)
                                gs = slice(t0 + ci * 64, t0 + (ci + 1) * 64)
                                pa, pares = PAr.next()
                                lo = slice(ci * 64, ci * 64 + 32)
                                hi = slice(ci * 64 + 32, ci * 64 + 64)
                                glo = slice(t0 + ci * 64, t0 + ci * 64 + 32)
                                ghi = slice(t0 + ci * 64 + 32, t0 + ci * 64 + 64)
                                sA, sB, gB = (lo, hi, ghi) if r == 0 else (hi, lo, glo)
                                S.op("pe", lambda e: e.matmul(pa[:, 0:64], lhsT=ke[:, sA], rhs=QET[r][:, gs],
                                                              start=True, stop=True, skip_group_check=True),
                                     reads=[keres, rQET[r][bidx]], writes=[pares])
                                S.op("pe", lambda e: e.matmul(pa[:, 64:96], lhsT=ke[:, sB], rhs=QET[r][:, gB],
                                                              start=False, stop=True, skip_group_check=True),
                                     reads=[keres, rQET[r][bidx]], writes=[pares])
                                if r == 0:
                                    mA = MSK[0:32, 2, 0:64]
                                    mB = MSK[0:32, 2, 0:32]
                                    oB = ATB[r][:, cg, 32:64]
                                else:
                                    mA = MSK[0:32, 4, 0:64]
                                    mB = MSK[0:32, 3, 0:32]
                                    oB = ATB[r][:, cg, 0:32]
                                S.op("dve", lambda e: e.tensor_tensor(out=ATA[r][:, cg, :], in0=pa[:, 0:64], in1=mA, op=ALU.mult),
                                     reads=[pares, rC], writes=[rAT[r][bidx]])
                                S.op("dve", lambda e: e.tensor_tensor(out=oB, in0=pa[:, 64:96], in1=mB, op=ALU.mult),
                                     reads=[pares, rC, rATB0], writes=[rAT[r][bidx]])
                                pk, pkres = PKr.next()
                                kc, kcres = KCr.next()
                                S.op("pe", (lambda pk=pk, ke=ke, cs=cs: lambda e: e.transpose(pk[:], ke[:, cs], IDN[:]))(),
                                     reads=[keres, rC], writes=[pkres])
                                S.op("act", (lambda pk=pk, kc=kc: lambda e: e.activation(out=kc[:], in_=pk[:], func=AF.Copy))(),
                                     reads=[pkres], writes=[kcres])
                                pu, pures = PUr.next()
                                S.op("pe", (lambda pu=pu, kc=kc, vb=vb, ci=ci: lambda e: e.matmul(
                                    pu[:], lhsT=kc[:], rhs=vb[:, ci, :], start=True, stop=True))(),
                                    reads=[kcres, vbres], writes=[pures])
                                sl = nslot
                                S.op("dve", lambda e: e.tensor_scalar(
                                    out=eu[:, sl, :], in0=pu[:], scalar1=ex[:, 2, ci:ci + 1], scalar2=None, op0=ALU.mult),
                                    reads=[pures, exres], writes=[eus[sl]])
                                S.op("act", lambda e: e.activation(out=sc[:, :, sl], in_=ex[:, 0:2, ci], func=AF.Copy),
                                     reads=[exres], writes=[scres])
                                seg_chunks.append(cg)
                                nslot += 1
                                done += 1
                                if nslot == SEGN or done == total:
                                    n = nslot
                                    cin, cinres = CARRY[car], rCAR[car]
                                    cout, coutres = CARRY[1 - car], rCAR[1 - car]
                                    sp_, spres = SPr.next()
                                    for j in range(n):
                                        prev, prevres = (cin[:], cinres) if j == 0 else (eu[:, j - 1, :], eus[j - 1])
                                        idx = j if r == 0 else n - 1 - j
                                        S.op("act", lambda e: e.activation(
                                            out=sp_[:, idx, :], in_=prev, func=AF.Copy, scale=sc[:, 1, j:j + 1]),
                                            reads=[prevres, scres], writes=[spres])
                                        S.op("dve", lambda e: e.scalar_tensor_tensor(
                                            out=eu[:, j, :], in0=prev, scalar=sc[:, 0, j:j + 1], in1=eu[:, j, :],
                                            op0=ALU.mult, op1=ALU.add), reads=[prevres, scres, eus[j]], writes=[eus[j]])
                                    S.op("pool", lambda e: e.tensor_copy(out=cout[:], in_=eu[:, n - 1, :]),
                                         reads=[eus[n - 1]], writes=[coutres])
                                    order_c = seg_chunks if r == 0 else seg_chunks[::-1]
                                    i0 = 0
                                    while i0 < n:
                                        i1 = i0
                                        while i1 + 1 < n and order_c[i1 + 1] == order_c[i1] + 1:
                                            i1 += 1
                                        S.dma("sp", SPD[r, order_c[i0]:order_c[i1] + 1].rearrange("c k v -> k c v"),
                                              sp_[:, i0:i1 + 1, :], reads=[spres])
                                        i0 = i1 + 1
                                    car = 1 - car
                                    nslot = 0
                                    seg_chunks = []
                                    eu, eures = EUr.next()
                                    eus = EUS[(EUr.i - 1) % EUr.n]
                                    sc, scres = SCr.next()
                    S.flush()
                    if dbg:
                        for r in range(2):
                            S.dma("sp", DBGQ[h, r], QET[r][:], reads=[])
                            pass
                    for bidx, (t0, nch) in enumerate(blocks):
                        if t0 == 0 and not need_ctx:
                            continue
                        nb = nch * 64
                        vb, vbres = VBr.next()
                        gg, ggres = GGr.next()
                        yb, ybres = YBr.next()
                        vh, vhres = VHr.next()
                        S.dma("sp", vh[:, 0:2 * nch, :], HV[t0:t0 + nb, h * 128:(h + 1) * 128].rearrange(
                            "(c s) d -> s c d", s=32), writes=[vhres])
                        S.dma("sp", gg[:, 0:nch, :], HG[t0:t0 + nb, h * 128:(h + 1) * 128].rearrange(
                            "(c s) d -> s c d", s=64), writes=[ggres])
                        sfs = []
                        for r in range(2):
                            sf, sfres = SFr.next()
                            S.dma("sp", sf[:, 0:nch, :], SPD[r, t0 // 64:t0 // 64 + nch].rearrange("c k v -> k c v"),
                                  writes=[sfres])
                            sfs.append((sf, sfres))
                        for ci in range(nch):
                            cg = t0 // 64 + ci
                            gs = slice(t0 + ci * 64, t0 + (ci + 1) * 64)
                            po, pores = POr.next()
                            for r in range(2):
                                sf, sfres = sfs[r]
                                S.op("pe", (lambda po=po, sf=sf, ci=ci, gs=gs, r=r: lambda e: e.matmul(
                                    po[:], lhsT=QET[r][:, gs], rhs=sf[:, ci, :], start=(r == 0), stop=False))(),
                                    reads=[rQET[r][bidx], sfres], writes=[pores])
                                hA, hB = (0, 1) if r == 0 else (1, 0)
                                S.op("pe", lambda e: e.matmul(po[:], lhsT=ATA[r][:, cg, :], rhs=vh[:, 2 * ci + hA, :],
                                                              start=False, stop=False),
                                     reads=[rAT[r][bidx], vhres], writes=[pores])
                                S.op("pe", lambda e: e.matmul(po[:], lhsT=ATB[r][:, cg, :], rhs=vh[:, 2 * ci + hB, :],
                                                              start=False, stop=(r == 1)),
                                     reads=[rAT[r][bidx], vhres], writes=[pores])
                            rs, rsres = RSr.next()
                            jk, jkres = JKr.next()
                            y1, y1res = Y1r.next()
                            S.op("act", (lambda po=po, jk=jk, rs=rs: lambda e: e.activation(
                                out=jk[:], in_=po[:], func=AF.Square, accum_out=rs[:, 0:1]))(),
                                reads=[pores], writes=[jkres, rsres])
                            S.op("act", (lambda rs=rs: lambda e: e.activation(
                                out=rs[:, 1:2], in_=rs[:, 0:1], func=AF.Sqrt, scale=1.0 / 128, bias=EPS))(),
                                reads=[rsres], writes=[rsres])
                            S.op("dve", (lambda rs=rs: lambda e: e.reciprocal(out=rs[:, 2:3], in_=rs[:, 1:2]))(),
                                 reads=[rsres], writes=[rsres])
                            S.op("dve", (lambda po=po, rs=rs, y1=y1: lambda e: e.scalar_tensor_tensor(
                                out=y1[:], in0=po[:], scalar=rs[:, 2:3], in1=HNW[0:64, :], op0=ALU.mult, op1=ALU.mult))(),
                                reads=[pores, rsres, rC], writes=[y1res])
                            S.op("pool", (lambda y1=y1, gg=gg, yb=yb, ci=ci: lambda e: e.tensor_tensor(
                                out=yb[:, ci, :], in0=y1[:], in1=gg[:, ci, :], op=ALU.mult))(),
                                reads=[y1res, ggres], writes=[ybres])
                        S.dma("sp", YB[t0:t0 + nb, h * 128:(h + 1) * 128].rearrange("(c s) d -> s c d", s=64),
                              yb[:, 0:nch, :], reads=[ybres])
                    S.flush()

        def phase_P5a(l):
            need_ctx = l < L - 1
            xsrc = xcat if l == 0 else XS
            with ExitStack() as st:
                WO = sbt(st, "WO", [128, 8, D], BF16)
                rWO = Res()
                S.dma("sp", WO[:], WM_out[l], writes=[rWO])
                YIr = Ring(nc, st, "yi", 3, [128, 512], BF16)
                PTr = Ring(nc, st, "ptb", 2, [128, 8, 128], BF16, psum=True)
                YTr = [Ring(nc, st, "yt%d" % b, 2, [128, 4, 512], BF16) for b in range(3)]
                WBr = Ring(nc, st, "wbr", 3, [128, 3, 4, 128], BF16)
                GTr = Ring(nc, st, "gt", 3, [128, 3, 512], BF16)
                PSr = Ring(nc, st, "psb", 4, [128, 512], F32, psum=True)
                M0r = Ring(nc, st, "m0", 2, [128, 512], F32)
                M1r = Ring(nc, st, "m1", 2, [128, 512], F32)
                M2r = Ring(nc, st, "m2", 2, [128, 512], F32)
                MTr = Ring(nc, st, "mt", 2, [128, 8, 512], BF16)
                XR = Ring(nc, st, "x5", 2, [128, D], F32)
                TTr = Ring(nc, st, "tt", 2, [128, D], F32)
                X1r = Ring(nc, st, "x1", 2, [128, D], F32)
                JK = Ring(nc, st, "jk5", 1, [128, D], BF16)
                SSr = Ring(nc, st, "ss5", 4, [128, 4], F32)
                S2r = Ring(nc, st, "s25", 4, [128, 4], F32)
                H1r = Ring(nc, st, "h15", 2, [128, D], F32)
                Hr = Ring(nc, st, "hb5", 2, [128, D], BF16)
                HTr = Ring(nc, st, "ht5", 2, [128, 8, 512], BF16)
                rings = (JK, SSr, H1r, Hr, PTr)
                MODp = sbt(st, "MODp5", [128, 2, 3, D], F32)
                rMOD = Res()
                S.dma("sp", MODp[:], MODD[:, :, 2:5, :], writes=[rMOD])
                for (t0, ntok) in supers:
                    if t0 == 0 and not need_ctx:
                        continue
                    mset = 1 if t0 == 0 else 0
                    nt = ntok // 128
                    yts = [YTr[b].next() for b in range(3)]
                    S.dma("sp", yts[0][0][:, :, 0:ntok], YAT[:, t0:t0 + ntok].rearrange("(k p) t -> p k t", p=128),
                          writes=[yts[0][1]])
                    for b, ysrc in ((1, YB), (2, YC)):
                        yt, ytres = yts[b]
                        for j in range(nt):
                            yi, yires = YIr.next()
                            S.dma("sp", yi[:], ysrc[t0 + j * 128:t0 + (j + 1) * 128, :], writes=[yires])
                            pt, ptres = PTr.next()
                            for k in range(4):
                                S.op("pe", (lambda pt=pt, yi=yi, k=k: lambda e: e.transpose(
                                    pt[:, k, :], yi[:, k * 128:(k + 1) * 128], IDN[:]))(), reads=[yires, rC], writes=[ptres])
                            S.op("act", (lambda yt=yt, pt=pt, j=j: lambda e: e.activation(
                                out=yt[:, :, j * 128:(j + 1) * 128], in_=pt[:, 0:4, :], func=AF.Copy))(),
                                reads=[ptres], writes=[ytres])
                    mt, mtres = MTr.next()
                    for ob in range(8):
                        wb, wbres = WBr.next()
                        S.dma("sp", wb[:], WS_br[l, :, ob].rearrange("j p k c -> p j k c"), writes=[wbres])
                        gt, gtres = GTr.next()
                        S.dma("sp", gt[:, :, 0:ntok], GT[:, t0:t0 + ntok].rearrange("(j o p) t -> o p j t", j=3, p=128)[ob],
                              writes=[gtres])
                        ms = []
                        for b in range(3):
                            yt, ytres = yts[b]
                            ps, pres = PSr.next()
                            for k in range(4):
                                S.op("pe", (lambda ps=ps, wb=wb, yt=yt, b=b, k=k: lambda e: e.matmul(
                                    ps[:, 0:ntok], lhsT=wb[:, b, k, :], rhs=yt[:, k, 0:ntok], start=(k == 0), stop=(k == 3)))(),
                                    reads=[wbres, ytres], writes=[pres])
                            m, mres = (M0r, M1r, M2r)[b].next()
                            S.op("dve", (lambda ps=ps, m=m, gt=gt, b=b: lambda e: e.tensor_tensor(
                                out=m[:, 0:ntok], in0=ps[:, 0:ntok], in1=gt[:, b, 0:ntok], op=ALU.mult))(),
                                reads=[pres, gtres], writes=[mres])
                            ms.append((m, mres))
                        (m0, m0r), (m1, m1r), (m2, m2r) = ms
                        S.op("pool", (lambda m0=m0, m1=m1: lambda e: e.tensor_tensor(
                            out=m0[:, 0:ntok], in0=m0[:, 0:ntok], in1=m1[:, 0:ntok], op=ALU.add))(),
                            reads=[m0r, m1r], writes=[m0r])
                        S.op("pool", (lambda m0=m0, m2=m2, mt=mt, ob=ob: lambda e: e.tensor_tensor(
                            out=mt[:, ob, 0:ntok], in0=m0[:, 0:ntok], in1=m2[:, 0:ntok], op=ALU.add))(),
                            reads=[m0r, m2r], writes=[mtres])
                    HTt, HTres = HTr.next()
                    for j in range(nt):
                        tk = slice(t0 + j * 128, t0 + (j + 1) * 128)
                        xt, xres = XR.next()
                        S.dma("sp", xt[:], xsrc[tk, :], writes=[xres])
                        tt, ttres = TTr.next()
                        s2, s2res = S2r.next()
                        jk, jkres = JK.next()
                        pss = []
                        for nb in range(2):
                            ps, pres = PSr.next()
                            for k in range(8):
                                S.op("pe", (lambda ps=ps, mt=mt, j=j, k=k, nb=nb: lambda e: e.matmul(
                                    ps[:], lhsT=mt[:, k, j * 128:(j + 1) * 128], rhs=WO[:, k, nb * 512:(nb + 1) * 512],
                                    start=(k == 0), stop=(k == 7)))(), reads=[mtres, rWO], writes=[pres])
                            S.op("act", (lambda ps=ps, jk=jk, s2=s2, nb=nb: lambda e: e.activation(
                                out=jk[:, nb * 512:(nb + 1) * 512], in_=ps[:], func=AF.Square, accum_out=s2[:, nb:nb + 1]))(),
                                reads=[pres], writes=[jkres, s2res])
                            pss.append((ps, pres))
                        S.op("dve", (lambda s2=s2: lambda e: e.tensor_tensor(out=s2[:, 2:3], in0=s2[:, 0:1], in1=s2[:, 1:2],
                                                                             op=ALU.add))(), reads=[s2res], writes=[s2res])
                        S.op("act", (lambda s2=s2: lambda e: e.activation(out=s2[:, 3:4], in_=s2[:, 2:3], func=AF.Sqrt,
                                                                          scale=1.0 / D, bias=EPS))(), reads=[s2res], writes=[s2res])
                        S.op("dve", (lambda s2=s2: lambda e: e.reciprocal(out=s2[:, 0:1], in_=s2[:, 3:4]))(),
                             reads=[s2res], writes=[s2res])
                        for nb, (ps, pres) in enumerate(pss):
                            S.op("dve", (lambda ps=ps, s2=s2, tt=tt, nb=nb: lambda e: e.scalar_tensor_tensor(
                                out=tt[:, nb * 512:(nb + 1) * 512], in0=ps[:], scalar=s2[:, 0:1],
                                in1=MODp[:, mset, 0, nb * 512:(nb + 1) * 512], op0=ALU.mult, op1=ALU.mult))(),
                                reads=[pres, s2res, rMOD], writes=[ttres])
                        x1, x1res = X1r.next()
                        S.op("pool", (lambda x1=x1, xt=xt, tt=tt: lambda e: e.tensor_tensor(
                            out=x1[:], in0=xt[:], in1=tt[:], op=ALU.add))(), reads=[xres, ttres], writes=[x1res])
                        S.dma("sp", X1[tk, :], x1[:], reads=[x1res])
                        norm_mod_transpose(x1[:], x1res, MODp[:, mset, 2, :], MODp[:, mset, 1, :], rMOD, HTt, HTres, j, rings)
                    S.dma("sp", H2T[:, :, t0:t0 + ntok], HTt[:, :, 0:ntok], reads=[HTres])
                S.flush()

        def phase_P5b(l):
            need_ctx = l < L - 1
            last = (l == L - 1)
            with ExitStack() as st:
                HTr = Ring(nc, st, "h2", 2, [128, 8, 512], BF16)
                WUr = Ring(nc, st, "wu", 4, [128, 2, 8, 128], BF16)
                PSr = Ring(nc, st, "psu", 3, [128, 512], F32, psum=True)
                RLr = Ring(nc, st, "rl", 3, [128, 512], F32)
                ATr = Ring(nc, st, "at", 2, [128, 32, 512], BF16)
                WDr = Ring(nc, st, "wd", 4, [128, 4, D], BF16)
                PDr = Ring(nc, st, "pd", 4, [128, 512], F32, psum=True)
                XR = Ring(nc, st, "x6", 3, [128, D], F32)
                TTr = Ring(nc, st, "t6", 2, [128, D], F32)
                XOr = Ring(nc, st, "xo", 2, [128, D], F32)
                JK = Ring(nc, st, "jk6", 1, [128, D], BF16)
                S2r = Ring(nc, st, "s26", 4, [128, 4], F32)
                MODp = sbt(st, "MODp6", [128, 2, 1, D], F32)
                rMOD = Res()
                S.dma("sp", MODp[:], MODD[:, :, 5:6, :], writes=[rMOD])
                for (t0, ntok) in supers:
                    if t0 == 0 and not need_ctx:
                        continue
                    mset = 1 if t0 == 0 else 0
                    nt = ntok // 128
                    ht, htres = HTr.next()
                    S.dma("sp", ht[:, :, 0:ntok], H2T[:, :, t0:t0 + ntok], writes=[htres])
                    at, atres = ATr.next()
                    for g0 in range(0, 32, 2):
                        wu, wures = WUr.next()
                        S.dma("sp", wu[:], WS_up[l, g0:g0 + 2].rearrange("b p k c -> p b k c"), writes=[wures])
                        for b in range(2):
                            ps, pres = PSr.next()
                            for k in range(8):
                                S.op("pe", (lambda ps=ps, wu=wu, b=b, k=k: lambda e: e.matmul(
                                    ps[:, 0:ntok], lhsT=wu[:, b, k, :], rhs=ht[:, k, 0:ntok], start=(k == 0), stop=(k == 7)))(),
                                    reads=[wures, htres], writes=[pres])
                            rl, rlres = RLr.next()
                            S.op("act", (lambda ps=ps, rl=rl: lambda e: e.activation(
                                out=rl[:, 0:ntok], in_=ps[:, 0:ntok], func=AF.Relu))(), reads=[pres], writes=[rlres])
                            S.op("pool", (lambda rl=rl, at=at, fb=g0 + b: lambda e: e.tensor_tensor(
                                out=at[:, fb, 0:ntok], in0=rl[:, 0:ntok], in1=rl[:, 0:ntok], op=ALU.mult))(),
                                reads=[rlres], writes=[atres])
                    for jp in range(0, nt, 2):
                        js = list(range(jp, min(jp + 2, nt)))
                        acc = {}
                        for j in js:
                            for nb in range(2):
                                acc[(j, nb)] = PDr.next()
                        for k0 in range(0, 32, 4):
                            wd, wdres = WDr.next()
                            S.dma("sp", wd[:], WM_dn[l, :, k0:k0 + 4, :], writes=[wdres])
                            for j in js:
                                for nb in range(2):
                                    ps, pres = acc[(j, nb)]
                                    for k in range(4):
                                        S.op("pe", (lambda ps=ps, at=at, wd=wd, j=j, k=k, k0=k0, nb=nb: lambda e: e.matmul(
                                            ps[:], lhsT=at[:, k0 + k, j * 128:(j + 1) * 128], rhs=wd[:, k, nb * 512:(nb + 1) * 512],
                                            start=(k0 + k == 0), stop=(k0 + k == 31)))(), reads=[atres, wdres], writes=[pres])
                        for j in js:
                            tk = slice(t0 + j * 128, t0 + (j + 1) * 128)
                            xt, xres = XR.next()
                            S.dma("sp", xt[:], X1[tk, :], writes=[xres])
                            tt, ttres = TTr.next()
                            s2, s2res = S2r.next()
                            jk, jkres = JK.next()
                            for nb in range(2):
                                ps, pres = acc[(j, nb)]
                                S.op("act", (lambda ps=ps, jk=jk, s2=s2, nb=nb: lambda e: e.activation(
                                    out=jk[:, nb * 512:(nb + 1) * 512], in_=ps[:], func=AF.Square,
                                    accum_out=s2[:, nb:nb + 1]))(), reads=[pres], writes=[jkres, s2res])
                            S.op("dve", (lambda s2=s2: lambda e: e.tensor_tensor(out=s2[:, 2:3], in0=s2[:, 0:1], in1=s2[:, 1:2],
                                                                                 op=ALU.add))(), reads=[s2res], writes=[s2res])
                            S.op("act", (lambda s2=s2: lambda e: e.activation(out=s2[:, 3:4], in_=s2[:, 2:3], func=AF.Sqrt,
                                                                              scale=1.0 / D, bias=EPS))(), reads=[s2res], writes=[s2res])
                            S.op("dve", (lambda s2=s2: lambda e: e.reciprocal(out=s2[:, 0:1], in_=s2[:, 3:4]))(),
                                 reads=[s2res], writes=[s2res])
                            for nb in range(2):
                                ps, pres = acc[(j, nb)]
                                S.op("dve", (lambda ps=ps, s2=s2, tt=tt, nb=nb: lambda e: e.scalar_tensor_tensor(
                                    out=tt[:, nb * 512:(nb + 1) * 512], in0=ps[:], scalar=s2[:, 0:1],
                                    in1=MODp[:, mset, 0, nb * 512:(nb + 1) * 512], op0=ALU.mult, op1=ALU.mult))(),
                                    reads=[pres, s2res, rMOD], writes=[ttres])
                            xo, xores = XOr.next()
                            S.op("pool", (lambda xo=xo, xt=xt, tt=tt: lambda e: e.tensor_tensor(
                                out=xo[:], in0=xt[:], in1=tt[:], op=ALU.add))(), reads=[xres, ttres], writes=[xores])
                            if last:
                                S.dma("sp", out[t0 - CTX + j * 128:t0 - CTX + (j + 1) * 128, :], xo[:], reads=[xores])
                            else:
                                S.dma("sp", XS[tk, :], xo[:], reads=[xores])
                S.flush()

        plan = [("W", l) for l in range(L)]
        for l in range(L):
            plan += [("M", l), ("P1", l), ("P2", l), ("P4", l), ("P3", l), ("P5a", l), ("P5b", l)]
        fns = dict(W=phase_W, M=phase_M, P1=phase_P1, P2=phase_P2, P4=phase_P4, P3=phase_P3, P5a=phase_P5a, P5b=phase_P5b)
        for (nm, l) in plan:
            fns[nm](l)
            if stop == "%s_%d" % (nm, l):
                break
        S.op("pool", lambda e: e.memset(LAM[:, 3:4], 0.0), writes=[rC])
        S.flush(final=True)
    return nc


_CACHE = {}


def make_consts(SEQ):
    ident = np.eye(128, dtype=np.float32).astype(ml_dtypes.bfloat16)
    j = np.arange(128)[:, None]
    i = np.arange(128)[None, :]
    masks = np.stack([(i <= j), (j <= i), (j <= i), (j >= i), (j + 32 >= i)]).astype(np.float32).astype(ml_dtypes.bfloat16)
    reset = np.ones((128, 512), np.float32)
    reset[:, ::64] = 0.0
    return dict(rope=rope_tables(SEQ), ident=ident, masks=masks, reset=reset)


def host_inputs(SEQ, b, x, c, ctx, c_ctx, w_ada, b_ada, norm_g, w_in, diff_lambda, diff_subln,
                hgrn_lb, hgrn_norm, swa_sink, w_branch, w_out, w_mlp_up, w_mlp_down, shared):
    f = lambda a: np.ascontiguousarray(np.asarray(a, dtype=np.float32))
    m = dict(shared)
    m["xcat"] = np.ascontiguousarray(np.concatenate([np.asarray(ctx[b]), np.asarray(x[b])], axis=0).astype(np.float32))
    m["c2"] = np.ascontiguousarray(np.stack([np.asarray(c[b]), np.asarray(c_ctx)]).astype(np.float32))
    return m


def shared_inputs(SEQ, w_ada, b_ada, norm_g, w_in, diff_lambda, diff_subln, hgrn_lb, hgrn_norm, swa_sink,
                  w_branch, w_out, w_mlp_up, w_mlp_down):
    f = lambda a: np.ascontiguousarray(np.asarray(a, dtype=np.float32))
    scol, mcol = w_in_column_maps()
    w_in = np.asarray(w_in, dtype=np.float32)
    L = w_in.shape[0]
    m = dict(make_consts(SEQ))
    m.update(w_ada=f(w_ada), b_ada=f(b_ada), norm_g=f(np.asarray(norm_g).reshape(L, 4 * D)),
             w_in_s=np.ascontiguousarray(w_in[:, :, scol]), w_in_m=np.ascontiguousarray(w_in[:, :, mcol]),
             dlam=f(np.asarray(diff_lambda).reshape(L, 256)), dsub=f(diff_subln), hlb=f(hgrn_lb), hnorm=f(hgrn_norm),
             sink=f(swa_sink), w_br=f(w_branch), w_out=f(w_out), w_up=f(w_mlp_up), w_dn=f(w_mlp_down))
    return m


def kernel(x, c, ctx, c_ctx, w_ada, b_ada, norm_g, w_in, diff_lambda, diff_subln,
           hgrn_lb, hgrn_norm, swa_sink, w_branch, w_out, w_mlp_up, w_mlp_down):
    x = np.asarray(x)
    B, SEQ, _ = x.shape
    if SEQ not in _CACHE:
        _CACHE[SEQ] = build_nc(SEQ)
    nc = _CACHE[SEQ]
    shared = shared_inputs(SEQ, w_ada, b_ada, norm_g, w_in, diff_lambda, diff_subln, hgrn_lb, hgrn_norm,
                           swa_sink, w_branch, w_out, w_mlp_up, w_mlp_down)
    ncores = B
    in_maps = []
    for core in range(ncores):
        b = core % B
        in_maps.append(host_inputs(SEQ, b, x, c, ctx, c_ctx, w_ada, b_ada, norm_g, w_in, diff_lambda, diff_subln,
                                   hgrn_lb, hgrn_norm, swa_sink, w_branch, w_out, w_mlp_up, w_mlp_down, shared))
    res = run_bass_kernel_spmd(nc, in_maps, core_ids=list(range(ncores)))
    outs = [np.asarray(res.results[b]["out"], dtype=np.float32) for b in range(B)]
    return np.stack(outs, axis=0)
```
